# Optimizing a Trainium2 kernel written in Bass

```python
import jax, jax.numpy as jnp
from jax import lax
import numpy as np

D_MODEL = 2048
BATCH = 4
SEQ = 4096
DEPTH = 1

MEM_LEN = 256
POOL_WIDTH = 1024
POOL_WINDOWS = (2, 4, 8, 16)
POOL_GROUPS = len(POOL_WINDOWS)
POOL_GROUP = POOL_WIDTH // POOL_GROUPS
ATTN_HEADS = 8
HEAD_DIM = 128
ATTN_WIDTH = ATTN_HEADS * HEAD_DIM
MIX_WIDTH = POOL_WIDTH + ATTN_WIDTH
IN_COLS = POOL_WIDTH + 3 * ATTN_WIDTH
DILATED_PAIRS = ((128, 1), (512, 4), (2048, 16))
BLOCK = 128
ROPE_THETA = 500000.0
ROPE_DIM = HEAD_DIM // 4
CROSS_HEADS = 4
CROSS_HEAD_DIM = 128
CROSS_WIDTH = CROSS_HEADS * CROSS_HEAD_DIM
PEER_KEYS = 128
PEER_EXPERTS = PEER_KEYS * PEER_KEYS
PEER_HEADS = 8
PEER_QUERY_DIM = 256
PEER_HALF = PEER_QUERY_DIM // 2
PEER_TOPK = 16
PEER_TOKEN_BLOCK = 128
EPS = 1e-6

kernel_name = "hybrid_pool_dilated_peer_block"


def rmsnorm(x, g):
    xf = x.astype(jnp.float32)
    y = xf * lax.rsqrt(jnp.mean(xf * xf, axis=-1, keepdims=True) + EPS)
    return (y * g.astype(jnp.float32)).astype(x.dtype)


def rope_partial(t, positions):
    half = ROPE_DIM // 2
    inv = ROPE_THETA ** (-jnp.arange(0, ROPE_DIM, 2, dtype=jnp.float32) / ROPE_DIM)
    ang = positions.astype(jnp.float32)[..., None] * inv
    cos = jnp.cos(ang)[:, :, None, :]
    sin = jnp.sin(ang)[:, :, None, :]
    tr = t[..., :ROPE_DIM].astype(jnp.float32)
    x1, x2 = tr[..., :half], tr[..., half:]
    rot = jnp.concatenate([x1 * cos - x2 * sin, x2 * cos + x1 * sin], axis=-1)
    return jnp.concatenate([rot.astype(t.dtype), t[..., ROPE_DIM:]], axis=-1)


def causal_pool_mixer(p, w_pool, pool_scale):
    B, S, _ = p.shape
    pg = p.reshape(B, S, POOL_GROUPS, POOL_GROUP)
    c = jnp.cumsum(pg.astype(jnp.float32), axis=1)
    t = jnp.arange(S)
    pooled = []
    for g, w in enumerate(POOL_WINDOWS):
        cg = c[:, :, g]
        prev = jnp.pad(cg, ((0, 0), (w, 0), (0, 0)))[:, :S]
        cnt = jnp.minimum(t + 1, w).astype(jnp.float32)[None, :, None]
        pooled.append((cg - prev) / cnt)
    mixed = (jnp.stack(pooled, axis=2) - pg.astype(jnp.float32)).astype(p.dtype)
    y = jnp.einsum('bsgc,gce->bsge', mixed, w_pool) * pool_scale
    return y.reshape(B, S, POOL_WIDTH)


def dilated_branch(q, k, v, window, dilation):
    B, S, H, Dh = q.shape
    L = S // dilation
    w_sub = window // dilation
    nb = -(-L // BLOCK)
    Lp = nb * BLOCK

    def to_sub(t):
        t = t.reshape(B, L, dilation, H, Dh).transpose(0, 2, 3, 1, 4).reshape(B * dilation, H, L, Dh)
        t = jnp.pad(t, ((0, 0), (0, 0), (0, Lp - L), (0, 0)))
        return t.reshape(B * dilation, H, nb, BLOCK, Dh)

    def with_prev(t):
        prev = jnp.pad(t, ((0, 0), (0, 0), (1, 0), (0, 0), (0, 0)))[:, :, :nb]
        return jnp.concatenate([prev, t], axis=3)

    qs = to_sub(q)
    kk = with_prev(to_sub(k))
    vv = with_prev(to_sub(v))
    s = jnp.einsum('ghnqd,ghnkd->ghnqk', qs, kk).astype(jnp.float32)
    ql = jnp.arange(BLOCK)[:, None]
    kl = jnp.arange(2 * BLOCK)[None, :]
    dist = ql + BLOCK - kl
    key_idx = jnp.arange(nb)[:, None, None] * BLOCK - BLOCK + kl[None]
    mask = (dist >= 0) & (dist <= w_sub) & (key_idx >= 0)
    s = jnp.where(mask, s, -jnp.inf)
    m = jnp.max(s, axis=-1, keepdims=True)
    pe = jnp.exp(s - m)
    l = jnp.sum(pe, axis=-1, keepdims=True)
    o = jnp.einsum('ghnqk,ghnkd->ghnqd', pe, vv.astype(jnp.float32)) / l
    lse = (m + jnp.log(l))[..., 0]
    o = o.reshape(B, dilation, H, Lp, Dh)[:, :, :, :L].transpose(0, 3, 1, 2, 4).reshape(B, S, H, Dh)
    lse = lse.reshape(B, dilation, H, Lp)[..., :L].transpose(0, 3, 1, 2).reshape(B, S, H)
    return o, lse


def dilated_attention(q, k, v):
    outs, lses = [], []
    for window, dilation in DILATED_PAIRS:
        o, lse = dilated_branch(q, k, v, window, dilation)
        outs.append(o)
        lses.append(lse)
    wts = jax.nn.softmax(jnp.stack(lses, axis=-1), axis=-1)
    o = sum(wts[..., i, None] * outs[i] for i in range(len(outs)))
    return o.astype(q.dtype)


def memory_cross_attention(h, memn, w_cq, w_ck, w_cv, w_co):
    B, S, _ = h.shape
    M = memn.shape[1]
    q = (h @ w_cq).reshape(B, S, CROSS_HEADS, CROSS_HEAD_DIM) * (CROSS_HEAD_DIM ** -0.5)
    k = (memn @ w_ck).reshape(B, M, CROSS_HEADS, CROSS_HEAD_DIM)
    v = (memn @ w_cv).reshape(B, M, CROSS_HEADS, CROSS_HEAD_DIM)
    s = jnp.einsum('bshd,bmhd->bhsm', q, k).astype(jnp.float32)
    p = jax.nn.softmax(s, axis=-1).astype(v.dtype)
    o = jnp.einsum('bhsm,bmhd->bshd', p, v).reshape(B, S, CROSS_WIDTH)
    return o @ w_co


def peer_ffn(h, w_pq, sub_keys_1, sub_keys_2, w_u, w_v):
    B, S, D = h.shape
    T = B * S
    hf = h.reshape(T, D)
    q = (hf @ w_pq).reshape(T, PEER_HEADS, 2, PEER_HALF)
    s1 = jnp.einsum('thd,kd->thk', q[:, :, 0], sub_keys_1).astype(jnp.float32)
    s2 = jnp.einsum('thd,kd->thk', q[:, :, 1], sub_keys_2).astype(jnp.float32)
    v1, i1 = lax.top_k(s1, PEER_TOPK)
    v2, i2 = lax.top_k(s2, PEER_TOPK)
    cand = (v1[..., :, None] + v2[..., None, :]).reshape(T, PEER_HEADS, PEER_TOPK * PEER_TOPK)
    sc, ci = lax.top_k(cand, PEER_TOPK)
    e1 = jnp.take_along_axis(i1, ci // PEER_TOPK, axis=-1)
    e2 = jnp.take_along_axis(i2, ci % PEER_TOPK, axis=-1)
    experts = e1 * PEER_KEYS + e2
    gates = jax.nn.softmax(sc, axis=-1).astype(h.dtype)
    nblk = T // PEER_TOKEN_BLOCK
    idx = experts.reshape(nblk, PEER_TOKEN_BLOCK, PEER_HEADS * PEER_TOPK)
    gts = gates.reshape(nblk, PEER_TOKEN_BLOCK, PEER_HEADS * PEER_TOPK)

    def expert_block(args):
        hb, ib, gb = args
        u = jnp.take(w_u, ib, axis=0)
        a = jnp.einsum('td,tkd->tk', hb, u)
        c = gb * jax.nn.gelu(a)
        vv = jnp.take(w_v, ib, axis=0)
        return jnp.einsum('tk,tkd->td', c, vv)

    y = lax.map(expert_block, (hf.reshape(nblk, PEER_TOKEN_BLOCK, D), idx, gts))
    return y.reshape(B, S, D)


def setup_inputs(seed: int = 0) -> dict:
    key = jax.random.key(seed)
    ks = jax.random.split(key, 24)
    f32 = jnp.float32

    def nrm(k, shape, scale):
        return jax.random.normal(k, shape, f32) * scale

    def gain(k, shape):
        return 1.0 + 0.02 * jax.random.normal(k, shape, f32)

    offset = jax.random.randint(ks[2], (BATCH, 1), 0, 1024, dtype=jnp.int32)
    positions = offset + jnp.arange(SEQ, dtype=jnp.int32)[None, :]
    return {
        "x": nrm(ks[0], (BATCH, SEQ, D_MODEL), 1.0),
        "mem": nrm(ks[1], (BATCH, MEM_LEN, D_MODEL), 1.0),
        "positions": positions,
        "g_mix": gain(ks[3], (DEPTH, D_MODEL)),
        "w_in": nrm(ks[4], (DEPTH, D_MODEL, IN_COLS), D_MODEL ** -0.5),
        "w_pool": nrm(ks[5], (DEPTH, POOL_GROUPS, POOL_GROUP, POOL_GROUP), POOL_GROUP ** -0.5),
        "pool_scale": 1.0 + 0.1 * jax.random.normal(ks[6], (DEPTH, POOL_GROUPS, POOL_GROUP), f32),
        "w_out": nrm(ks[7], (DEPTH, MIX_WIDTH, D_MODEL), MIX_WIDTH ** -0.5),
        "g_cross": gain(ks[8], (DEPTH, D_MODEL)),
        "g_mem": gain(ks[9], (DEPTH, D_MODEL)),
        "w_cq": nrm(ks[10], (DEPTH, D_MODEL, CROSS_WIDTH), D_MODEL ** -0.5),
        "w_ck": nrm(ks[11], (DEPTH, D_MODEL, CROSS_WIDTH), D_MODEL ** -0.5),
        "w_cv": nrm(ks[12], (DEPTH, D_MODEL, CROSS_WIDTH), D_MODEL ** -0.5),
        "w_co": nrm(ks[13], (DEPTH, CROSS_WIDTH, D_MODEL), CROSS_WIDTH ** -0.5),
        "g_ffn": gain(ks[14], (DEPTH, D_MODEL)),
        "w_pq": nrm(ks[15], (DEPTH, D_MODEL, PEER_HEADS * PEER_QUERY_DIM), D_MODEL ** -0.5),
        "sub_keys_1": nrm(ks[16], (DEPTH, PEER_KEYS, PEER_HALF), PEER_HALF ** -0.5),
        "sub_keys_2": nrm(ks[17], (DEPTH, PEER_KEYS, PEER_HALF), PEER_HALF ** -0.5),
        "w_u": nrm(ks[18], (DEPTH, PEER_EXPERTS, D_MODEL), D_MODEL ** -0.5),
        "w_v": nrm(ks[19], (DEPTH, PEER_EXPERTS, D_MODEL), (PEER_HEADS * PEER_TOPK) ** -0.5),
        "g_final": gain(ks[20], (D_MODEL,)),
    }


def reference(x, mem, positions, g_mix, w_in, w_pool, pool_scale, w_out, g_cross, g_mem,
              w_cq, w_ck, w_cv, w_co, g_ffn, w_pq, sub_keys_1, sub_keys_2, w_u, w_v, g_final):
    B, S, _ = x.shape
    for i in range(DEPTH):
        h = rmsnorm(x, g_mix[i])
        z = h @ w_in[i]
        p = z[..., :POOL_WIDTH]
        q = z[..., POOL_WIDTH:POOL_WIDTH + ATTN_WIDTH].reshape(B, S, ATTN_HEADS, HEAD_DIM)
        k = z[..., POOL_WIDTH + ATTN_WIDTH:POOL_WIDTH + 2 * ATTN_WIDTH].reshape(B, S, ATTN_HEADS, HEAD_DIM)
        v = z[..., POOL_WIDTH + 2 * ATTN_WIDTH:].reshape(B, S, ATTN_HEADS, HEAD_DIM)
        pool_out = causal_pool_mixer(p, w_pool[i], pool_scale[i])
        q = rope_partial(q, positions) * (HEAD_DIM ** -0.5)
        k = rope_partial(k, positions)
        attn_out = dilated_attention(q, k, v).reshape(B, S, ATTN_WIDTH)
        x = x + jnp.concatenate([pool_out, attn_out], axis=-1) @ w_out[i]
        memn = rmsnorm(mem, g_mem[i])
        x = x + memory_cross_attention(rmsnorm(x, g_cross[i]), memn, w_cq[i], w_ck[i], w_cv[i], w_co[i])
        x = x + peer_ffn(rmsnorm(x, g_ffn[i]), w_pq[i], sub_keys_1[i], sub_keys_2[i], w_u[i], w_v[i])
    return rmsnorm(x, g_final)
```

```python
from contextlib import ExitStack

import numpy as np
import ml_dtypes
import concourse.bass as bass
import concourse.mybir as mybir
from concourse.bass_utils import run_bass_kernel_spmd

F32 = mybir.dt.float32
BF16 = mybir.dt.bfloat16
I32 = mybir.dt.int32
U32 = mybir.dt.uint32
ALU = mybir.AluOpType
AF = mybir.ActivationFunctionType
AX = mybir.AxisListType

NCORES = 8
D = 2048
NT = 2048
NH = 2048
MAGIC = 12582912.0
TWO_PI = 6.283185307179586
PI_SAFE = 3.1415925


class Buf:
    __slots__ = ("name", "w", "r", "psum")

    def __init__(self, name="", psum=False):
        self.name = name
        self.w = {}
        self.r = {}
        self.psum = psum


class Counter:
    def __init__(self, name, step):
        self.name = name
        self.step = step
        self.val = 0
        self.sem = None


class Stream:
    def __init__(self, name):
        self.name = name
        self.ops = []
        self.seen = {}


class Prog:
    def __init__(self, nc, es):
        self.nc = nc
        self.streams = {n: Stream(n) for n in ("pe", "act", "dve", "pool", "sp")}
        self.counters = {}
        for n in ("pe", "act", "dve", "pool"):
            self._counter(n, 1, es)
        self.groups = {}
        for n, k in (("ld", 8), ("ld2", 4), ("st", 8), ("cld", 4), ("cst", 4)):
            self.groups[n] = [[self._counter(f"{n}{i}", 16, es) for i in range(k)], 0]

    def _counter(self, name, step, es):
        c = Counter(name, step)
        c.sem = es.enter_context(self.nc.semaphore("s_" + name))
        self.counters[name] = c
        return c

    def _emit(self, stream, counter, fn, reads, writes, pe_accum=False, pre=None):
        st = self.streams[stream]
        need = {}

        def req(c, v):
            if need.get(c, 0) < v:
                need[c] = v
        if pre is not None and pre[1] > 0:
            req(*pre)
        for b in reads:
            for c, v in b.w.items():
                req(c, v)
            if b.psum:
                for c, v in b.r.items():
                    if c is not counter:
                        req(c, v)
        for b in writes:
            for c, v in b.w.items():
                if pe_accum and c is counter:
                    continue
                if counter.step == 16 and c.step == 16:
                    continue
                req(c, v)
            for c, v in b.r.items():
                req(c, v)
        waits = []
        for c, v in need.items():
            if st.seen.get(c, 0) < v:
                st.seen[c] = v
                waits.append((c, v))
        counter.val += counter.step
        myv = counter.val
        st.ops.append((waits, fn, counter))
        for b in reads:
            if b.r.get(counter, 0) < myv:
                b.r[counter] = myv
        for b in writes:
            if counter.step == 16 and b.w and all(c.step == 16 for c in b.w):
                b.w[counter] = myv
            else:
                b.w = {counter: myv}
            b.r = {}
        return myv

    def op(self, eng, fn, reads=(), writes=(), pe_accum=False):
        return self._emit(eng, self.counters[eng], fn, reads, writes, pe_accum)

    def dma(self, queue, gname, fn, reads=(), writes=()):
        grp = self.groups[gname]
        c = grp[0][grp[1] % len(grp[0])]
        grp[1] += 1
        return self._emit(queue, c, fn, reads, writes, pre=(c, c.val))

    def barrier(self, exclude=()):
        for st in self.streams.values():
            waits = []
            for c in self.counters.values():
                if c.name.rstrip("0123456789") in exclude or c.val == 0:
                    continue
                if st.seen.get(c, 0) < c.val:
                    st.seen[c] = c.val
                    waits.append((c, c.val))
            st.ops.append((waits, None, None))

    def flush(self, es):
        block = es.enter_context(self.nc.Block())
        engs = {"pe": block.tensor, "act": block.scalar, "dve": block.vector,
                "pool": block.gpsimd, "sp": block.sync}
        for name, deco in engs.items():
            st = self.streams[name]
            ops = st.ops
            st.ops = []

            def body(e, ops=ops):
                for waits, fn, counter in ops:
                    for c, v in waits:
                        e.wait_ge(c.sem, v)
                    if fn is not None:
                        fn(e).then_inc(counter.sem, counter.step)
            deco(body)


class Tl:
    __slots__ = ("t", "b")

    def __init__(self, t, b):
        self.t = t
        self.b = b


class Rot:
    def __init__(self, items):
        self.items = items
        self.i = 0

    def next(self):
        x = self.items[self.i % len(self.items)]
        self.i += 1
        return x


class KB:
    def __init__(self, debug=False, upto=99):
        self.debug = debug
        self.upto = upto
        self.nc = bass.Bass("TRN2", target_bir_lowering=False)
        self.D = {}
        self.DB = {}

    def din(self, name, shape, dt):
        self.D[name] = self.nc.dram_tensor(name, list(shape), dt, kind="ExternalInput").ap()
        self.DB[name] = Buf(name)

    def dscr(self, name, shape, dt):
        kind = "ExternalOutput" if self.debug else "Internal"
        self.D[name] = self.nc.dram_tensor(name, list(shape), dt, kind=kind).ap()
        self.DB[name] = Buf(name)

    def dout(self, name, shape, dt):
        self.D[name] = self.nc.dram_tensor(name, list(shape), dt, kind="ExternalOutput").ap()
        self.DB[name] = Buf(name)

    def sb(self, es, name, shape, dt):
        self.uid = getattr(self, "uid", 0) + 1
        name = f"{name}_{self.uid}"
        t = es.enter_context(self.nc.sbuf_tensor(name, list(shape), dt))
        return Tl(t, Buf(name))

    def rot(self, es, name, shape, dt, n):
        return Rot([self.sb(es, f"{name}{i}", shape, dt) for i in range(n)])

    def psum(self, es, nf, nb):
        banks_f, banks_b = [], []
        self.uid = getattr(self, "uid", 0) + 1
        if nf:
            t = es.enter_context(self.nc.psum_tensor(f"psf_{self.uid}", [128, nf * 512], F32))
            bufs = [Buf(f"psf{i}", psum=True) for i in range(nf)]
            banks_f = (t, bufs)
        if nb:
            t2 = es.enter_context(self.nc.psum_tensor(f"psb_{self.uid}", [128, nb * 1024], BF16))
            bufs2 = [Buf(f"psb{i}", psum=True) for i in range(nb)]
            banks_b = (t2, bufs2)
        return banks_f, banks_b

    def mm(self, out, lhsT, rhs, start, stop, R, W):
        self.P.op("pe", lambda e: e.matmul(out, lhsT=lhsT, rhs=rhs, start=start, stop=stop,
                                          skip_group_check=True),
                  reads=R, writes=W, pe_accum=True)

    def tr(self, out, in_, ident, R, W):
        self.P.op("pe", lambda e: e.transpose(out=out, in_=in_, identity=ident),
                  reads=R, writes=W, pe_accum=True)

    def load(self, out, in_, R, W, cname="ld", queue="sp"):
        self.P.dma(queue, cname, lambda e: e.dma_start(out=out, in_=in_), reads=R, writes=W)

    def act_copy(self, out, in_, R, W, scale=None):
        if scale is None:
            self.P.op("act", lambda e: e.copy(out=out, in_=in_), reads=R, writes=W)
        else:
            self.P.op("act", lambda e: e.activation(out=out, in_=in_, func=AF.Copy, scale=scale),
                      reads=R, writes=W)

    def dve_copy(self, out, in_, R, W):
        self.P.op("dve", lambda e: e.tensor_copy(out=out, in_=in_), reads=R, writes=W)

    def tt(self, eng, out, in0, in1, op, R, W):
        self.P.op(eng, lambda e: e.tensor_tensor(out=out, in0=in0, in1=in1, op=op), reads=R, writes=W)

    def ts(self, eng, out, in0, s1, s2, op0, op1, R, W):
        if op1 is None:
            self.P.op(eng, lambda e: e.tensor_scalar(out=out, in0=in0, scalar1=s1, scalar2=None, op0=op0),
                      reads=R, writes=W)
        else:
            self.P.op(eng, lambda e: e.tensor_scalar(out=out, in0=in0, scalar1=s1, scalar2=s2, op0=op0, op1=op1),
                      reads=R, writes=W)

    def consts(self, es):
        c = {}
        c["idb"] = self.sb(es, "idb", [128, 128], BF16)
        self.load(c["idb"].t[:], self.D["ident_bf"][:, :], [self.DB["ident_bf"]], [c["idb"].b])
        return c

    def norm_scratch(self, es):
        s = {}
        s["junk"] = self.sb(es, "n_junk", [128, D], BF16)
        s["ss"] = self.rot(es, "n_ss", [128, 1], F32, 2)
        s["rs"] = self.rot(es, "n_rs", [128, 1], F32, 2)
        s["xn"] = self.rot(es, "n_xn", [128, D], BF16, 2)
        return s

    def norm_A(self, xt_ap, xt_b, ns):
        P = self.P
        ss = ns["ss"].next()
        rs = ns["rs"].next()
        xn = ns["xn"].next()
        junk = ns["junk"]
        P.op("act", lambda e: e.activation(out=junk.t[:], in_=xt_ap, func=AF.Square, accum_out=ss.t[:]),
             reads=[xt_b], writes=[junk.b, ss.b])
        P.op("act", lambda e: e.activation(out=rs.t[:], in_=ss.t[:], func=AF.Sqrt, scale=1.0 / D, bias=self.eps.t[:]),
             reads=[ss.b, self.eps.b], writes=[rs.b])
        P.op("dve", lambda e: e.reciprocal(out=rs.t[:], in_=rs.t[:]), reads=[rs.b], writes=[rs.b])
        self.ts("dve", xn.t[:], xt_ap, rs.t[:, 0:1], None, ALU.mult, None, [xt_b, rs.b], [xn.b])
        return xn

    def norm_B(self, xn, dst_fn, dst_b, idb, psb, bankA, bankB):
        pt, pbufs = psb
        for half, bank in ((0, bankA), (1, bankB)):
            for c in range(8):
                cc = half * 8 + c
                self.tr(pt[:, bank * 1024 + c * 128: bank * 1024 + (c + 1) * 128],
                        xn.t[:, cc * 128:(cc + 1) * 128], idb.t[:], [xn.b, idb.b], [pbufs[bank]])
            src = pt[:, bank * 1024:(bank + 1) * 1024].rearrange("p (c t) -> p c t", c=8)
            if half == 0:
                self.act_copy(dst_fn(0, 8), src, [pbufs[bank]], [dst_b])
            else:
                self.dve_copy(dst_fn(8, 16), src, [pbufs[bank]], [dst_b])

    def norm_T(self, xt_ap, xt_b, dst_fn, dst_b, ns, idb, psb, bankA, bankB):
        xn = self.norm_A(xt_ap, xt_b, ns)
        self.norm_B(xn, dst_fn, dst_b, idb, psb, bankA, bankB)

    def load_w(self, dst, src_fn, nck, ncols, g, stg):
        for c in range(nck):
            s = stg.next()
            self.load(s.t[:, 0:ncols], src_fn(c), [], [s.b], cname="ld2")
            if g is None:
                if c % 2 == 0:
                    self.act_copy(dst.t[:, c, :], s.t[:, 0:ncols], [s.b], [dst.b])
                else:
                    self.dve_copy(dst.t[:, c, :], s.t[:, 0:ncols], [s.b], [dst.b])
            else:
                if c % 2 == 0:
                    self.act_copy(dst.t[:, c, :], s.t[:, 0:ncols], [s.b, g.b], [dst.b], scale=g.t[:, c:c + 1])
                else:
                    self.ts("dve", dst.t[:, c, :], s.t[:, 0:ncols], g.t[:, c:c + 1], None, ALU.mult, None,
                            [s.b, g.b], [dst.b])

    def end_stage(self, es):
        self.P.barrier()
        self.P.flush(es)

    def declare(self):
        di, ds = self.din, self.dscr
        di("x_own", [NT, D], F32); di("x_halo", [NH, D], F32); di("mem", [256, D], F32)
        di("pos_t", [128, 32], I32)
        di("w_in", [D, 4096], F32); di("w_pool", [4, 256, 256], F32); di("pool_scale", [4, 256], F32)
        di("w_out", [D, D], F32)
        di("w_cq", [D, 512], F32); di("w_ck", [D, 512], F32); di("w_cv", [D, 512], F32); di("w_co", [512, D], F32)
        di("w_pq", [D, D], F32); di("skT1", [128, 128], F32); di("skT2", [128, 128], F32)
        di("w_uT", [D, 16384], F32); di("w_v", [16384, D], F32)
        for g in ("g_mix_t", "g_cross_t", "g_mem_t", "g_ffn_t"):
            di(g, [128, 16], F32)
        di("g_final", [1, D], F32)
        di("ident_bf", [128, 128], BF16); di("ident_f", [128, 128], F32); di("ones_bf", [128, 128], BF16)
        di("masks", [128, 3, 512], BF16); di("poolA", [128, 4, 3, 128], BF16)
        di("rope_c", [128, 2, 32], F32); di("iota128", [128, 128], F32); di("iota16", [128, 16], F32)
        di("eps_c", [128, 1], F32)
        ds("qT_d", [8, 128, NT], BF16); ds("kT_d", [8, 128, NT + NH], BF16); ds("v_d", [NT + NH, 1024], BF16)
        ds("mixT_d", [16, 128, NT], BF16)
        ds("kcT_d", [128, 4, 256], BF16); ds("vc_d", [128, 2, 512], BF16)
        ds("x2_d", [NT, D], F32); ds("h3T_d", [16, 128, NT], BF16); ds("route_d", [16, 128, 3, 128], F32)
        ds("wuT_bf", [D, 16384], BF16); ds("wv_bf", [16384, D], BF16)
        self.dout("out", [NT, D], F32)

    def build(self):
        nc = self.nc
        self.declare()
        self.cast_pos = 0
        with ExitStack() as top:
            self.P = Prog(nc, top)
            stages = [self.st_prep, self.st_inproj_pq, self.st_inproj_kv, self.st_attn, self.st_mem,
                      self.st_out_cross, self.st_route, self.st_experts]
            for i, st in enumerate(stages):
                if i > self.upto:
                    break
                with ExitStack() as es:
                    self.eps = self.sb(es, "eps", [128, 1], F32)
                    self.load(self.eps.t[:], self.D["eps_c"][:, :], [], [self.eps.b])
                    st(es)
                    self.end_stage(es)
            with ExitStack() as es:
                self.P.barrier()
                self.P.flush(es)
        return nc

    def st_prep(self, es):
        pass

    def cast_chunks(self, n, stg, cstb):
        Dm, DB = self.D, self.DB
        src_u = Dm["w_uT"].rearrange("r (a c) -> (r a) c", c=2048)
        dst_u = Dm["wuT_bf"].rearrange("r (a c) -> (r a) c", c=2048)
        for _ in range(n):
            k = self.cast_pos
            if k >= 256:
                return
            self.cast_pos += 1
            if k < 128:
                src, dst, nm = src_u[k * 128:(k + 1) * 128, :], dst_u[k * 128:(k + 1) * 128, :], "wuT_bf"
            else:
                k2 = k - 128
                src, dst, nm = Dm["w_v"][k2 * 128:(k2 + 1) * 128, :], Dm["wv_bf"][k2 * 128:(k2 + 1) * 128, :], "wv_bf"
            s_ = stg.next()
            self.load(s_.t[:, 0:2048], src, [], [s_.b], cname="cld")
            o = cstb.next()
            self.act_copy(o.t[:], s_.t[:, 0:2048], [s_.b], [o.b])
            self.load(dst, o.t[:], [o.b], [DB[nm]], cname="cst", queue="act")

    def rope_tables(self, es):
        Dm, DB = self.D, self.DB
        posi = self.sb(es, "posi", [128, 32], I32)
        posf = self.sb(es, "posf", [128, 32], F32)
        rc = self.sb(es, "rc", [128, 2, 32], F32)
        ang = self.sb(es, "ang", [128, 32, 32], F32)
        kk = self.sb(es, "kk", [128, 32, 32], F32)
        cs = self.sb(es, "cs", [128, 32, 32], F32)
        self.load(posi.t[:], Dm["pos_t"][:, :], [], [posi.b])
        self.load(rc.t[:], Dm["rope_c"][:, :, :], [], [rc.b])
        self.dve_copy(posf.t[:], posi.t[:], [posi.b], [posf.b])
        pb = posf.t[:].unsqueeze(2).to_broadcast([128, 32, 32])
        invb = rc.t[:, 0, :].unsqueeze(1).to_broadcast([128, 32, 32])
        phb = rc.t[:, 1, :].unsqueeze(1).to_broadcast([128, 32, 32])
        self.tt("dve", ang.t[:], pb, invb, ALU.mult, [posf.b, rc.b], [ang.b])
        self.tt("dve", ang.t[:], ang.t[:], phb, ALU.add, [ang.b, rc.b], [ang.b])
        self.ts("dve", kk.t[:], ang.t[:], 1.0 / TWO_PI, MAGIC, ALU.mult, ALU.add, [ang.b], [kk.b])
        self.ts("dve", kk.t[:], kk.t[:], MAGIC, None, ALU.subtract, None, [kk.b], [kk.b])
        self.P.op("dve", lambda e: e.scalar_tensor_tensor(out=ang.t[:], in0=kk.t[:], scalar=-TWO_PI, in1=ang.t[:],
                                                          op0=ALU.mult, op1=ALU.add),
                  reads=[kk.b, ang.b], writes=[ang.b])
        self.ts("dve", ang.t[:], ang.t[:], -PI_SAFE, PI_SAFE, ALU.max, ALU.min, [ang.b], [ang.b])
        self.P.op("act", lambda e: e.activation(out=cs.t[:], in_=ang.t[:], func=AF.Sin), reads=[ang.b], writes=[cs.b])
        return cs

    def rope_apply(self, zs, out, cs, gt, tmp):
        z3 = zs.t[:].rearrange("p (h d) -> p h d", h=8)
        o3 = out.t[:].rearrange("p (h d) -> p h d", h=8)
        sinb = cs.t[:, gt, 0:16].unsqueeze(1).to_broadcast([128, 8, 16])
        cosb = cs.t[:, gt, 16:32].unsqueeze(1).to_broadcast([128, 8, 16])
        x1 = z3[:, :, 0:16]
        x2 = z3[:, :, 16:32]
        t1, t2 = tmp.next(), tmp.next()
        self.tt("dve", t1.t[:], x1, cosb, ALU.mult, [zs.b, cs.b], [t1.b])
        self.tt("dve", t2.t[:], x2, sinb, ALU.mult, [zs.b, cs.b], [t2.b])
        self.tt("dve", o3[:, :, 0:16], t1.t[:], t2.t[:], ALU.subtract, [t1.b, t2.b], [out.b])
        t3, t4 = tmp.next(), tmp.next()
        self.tt("dve", t3.t[:], x2, cosb, ALU.mult, [zs.b, cs.b], [t3.b])
        self.tt("dve", t4.t[:], x1, sinb, ALU.mult, [zs.b, cs.b], [t4.b])
        self.tt("dve", o3[:, :, 16:32], t3.t[:], t4.t[:], ALU.add, [t3.b, t4.b], [out.b])
        self.dve_copy(o3[:, :, 32:128], z3[:, :, 32:128], [zs.b], [out.b])

    def st_inproj_pq(self, es):
        self.inproj(es, "pq")

    def st_inproj_kv(self, es):
        self.inproj(es, "kv")

    def inproj(self, es, mode):
        Dm, DB, P = self.D, self.DB, self.P
        col0 = 0 if mode == "pq" else 2048
        (pf, pfb), (pbt, pbb) = self.psum(es, 6, 2)
        psb = (pbt, pbb)
        c = self.consts(es)
        idb = c["idb"]
        ns = self.norm_scratch(es)
        cs = self.rope_tables(es)
        gt_ = self.sb(es, "gmix", [128, 16], F32)
        self.load(gt_.t[:], Dm["g_mix_t"][:, :], [], [gt_.b])
        W = self.sb(es, "W", [128, 16, 2048], BF16)
        stg = self.rot(es, "wstg", [128, 2048], F32, 2)
        self.load_w(W, lambda cc: Dm["w_in"][cc * 128:(cc + 1) * 128, col0:col0 + 2048], 16, 2048, gt_, stg)
        xts = self.rot(es, "xt", [128, D], F32, 3)
        xnTs = self.rot(es, "xnT", [128, 16, 128], BF16, 2)
        zss = self.rot(es, "zs", [128, 1024], F32, 2)
        rtok = self.rot(es, "rtok", [128, 1024], BF16, 2)
        tmp = self.rot(es, "rtmp", [128, 8, 16], F32, 4)
        tstage = self.rot(es, "tstage", [128, 8, 256], BF16, 2)
        if mode == "pq":
            ptok = self.rot(es, "ptok", [128, 1024], BF16, 3)
            A = self.sb(es, "poolA", [128, 4, 3, 128], BF16)
            self.load(A.t[:], Dm["poolA"][:, :, :, :], [], [A.b])
            wp = self.sb(es, "wp", [128, 4, 2, 256], BF16)
            wpf = self.sb(es, "wpf", [128, 4, 2, 256], F32)
            scb = self.sb(es, "scb", [128, 4, 256], F32)
            self.load(wpf.t[:].rearrange("p g k e -> p (g k) e"),
                      Dm["w_pool"].rearrange("g (k p) e -> p (g k) e", p=128), [], [wpf.b])
            self.load(scb.t[:].rearrange("p g e -> p (g e)"),
                      Dm["pool_scale"].rearrange("g e -> (g e)").partition_broadcast(128), [], [scb.b])
            for g_ in range(4):
                self.tt("dve", wp.t[:, g_, :, :], wpf.t[:, g_, :, :],
                        scb.t[:, g_, :].unsqueeze(1).to_broadcast([128, 2, 256]), ALU.mult, [wpf.b, scb.b], [wp.b])
            mixT = self.sb(es, "mixT", [128, 8, 128], BF16)
            postage = self.rot(es, "postage", [128, 8, 256], BF16, 2)
            tiles = [("halo", 15)] + [("own", i) for i in range(16)]
        else:
            vtok = self.rot(es, "vtok", [128, 1024], BF16, 2)
            tiles = [("halo", i) for i in range(16)] + [("own", i) for i in range(16)]

        p_prev = None
        cur_stage = None
        cur_po = None
        deferred = []
        nt_ = len(tiles)
        xt_of, xnT_of = {}, {}

        def issue_load(j):
            kind, ti = tiles[j]
            xsrc = Dm["x_halo"] if kind == "halo" else Dm["x_own"]
            xt = xts.next()
            xt_of[j] = xt
            self.load(xt.t[:], xsrc[ti * 128:(ti + 1) * 128, :], [], [xt.b])

        issue_load(0)
        if nt_ > 1:
            issue_load(1)
        xn_next = self.norm_A(xt_of[0].t[:], xt_of[0].b, ns)
        xnT_of[0] = xnTs.next()
        self.norm_B(xn_next, lambda a, b_, x_=xnT_of[0]: x_.t[:, a:b_, :], xnT_of[0].b, idb, psb, 0, 1)
        for j, (kind, ti) in enumerate(tiles):
            gt = ti if kind == "halo" else 16 + ti
            if j + 2 < nt_:
                issue_load(j + 2)
            for f in deferred:
                f()
            deferred = []
            xnT = xnT_of.pop(j)
            if j + 1 < nt_:
                xn_next = self.norm_A(xt_of[j + 1].t[:], xt_of[j + 1].b, ns)
            if mode == "pq" and kind == "halo":
                cbs = [0, 1]
            else:
                cbs = [0, 1, 2, 3]
            for cb in cbs:
                for cc in range(16):
                    self.mm(pf[:, cb * 512:(cb + 1) * 512], xnT.t[:, cc, :], W.t[:, cc, cb * 512:(cb + 1) * 512],
                            cc == 0, cc == 15, [xnT.b, W.b], [pfb[cb]])
            if j + 1 < nt_:
                xnT_of[j + 1] = xnTs.next()
                self.norm_B(xn_next, lambda a, b_, x_=xnT_of[j + 1]: x_.t[:, a:b_, :], xnT_of[j + 1].b, idb, psb, 0, 1)
                xt_of.pop(j, None)
            if mode == "pq":
                p_cur = ptok.next()
                self.dve_copy(p_cur.t[:], pf[:, 0:1024], [pfb[0], pfb[1]], [p_cur.b])
                if kind == "own":
                    zs = zss.next()
                    self.dve_copy(zs.t[:], pf[:, 1024:2048], [pfb[2], pfb[3]], [zs.b])
                    qtok = rtok.next()
                    self.rope_apply(zs, qtok, cs, gt, tmp)
                    if ti % 2 == 0:
                        cur_stage = tstage.next()
                        cur_po = postage.next()
                    for h in range(8):
                        self.tr(pbt[:, h * 128:(h + 1) * 128], qtok.t[:, h * 128:(h + 1) * 128], idb.t[:],
                                [qtok.b, idb.b], [pbb[0]])
                    self.act_copy(cur_stage.t[:, :, (ti % 2) * 128:(ti % 2 + 1) * 128],
                                  pbt[:, 0:1024].rearrange("p (h t) -> p h t", h=8), [pbb[0]], [cur_stage.b])
                    var = 0 if ti == 0 else 1
                    for cc in range(8):
                        g = cc // 2
                        bank = 4 + cc // 4
                        o = pf[:, 2048 + cc * 128: 2048 + (cc + 1) * 128]
                        self.mm(o, p_prev.t[:, cc * 128:(cc + 1) * 128], A.t[:, g, 2, :], cc % 4 == 0, False,
                                [p_prev.b, A.b], [pfb[bank]])
                        self.mm(o, p_cur.t[:, cc * 128:(cc + 1) * 128], A.t[:, g, var, :], False, True,
                                [p_cur.b, A.b], [pfb[bank]])
                    self.dve_copy(mixT.t[:], pf[:, 2048:3072].rearrange("p (c t) -> p c t", c=8),
                                  [pfb[4], pfb[5]], [mixT.b])
                    for ec in range(8):
                        g = ec // 2
                        bank = 4 + ec // 4
                        o = pf[:, 2048 + ec * 128: 2048 + (ec + 1) * 128]
                        for kc in range(2):
                            self.mm(o, wp.t[:, g, kc, (ec % 2) * 128:(ec % 2 + 1) * 128], mixT.t[:, 2 * g + kc, :],
                                    (ec % 4 == 0 and kc == 0), kc == 1, [wp.b, mixT.b], [pfb[bank]])
                    self.act_copy(cur_po.t[:, :, (ti % 2) * 128:(ti % 2 + 1) * 128],
                                  pf[:, 2048:3072].rearrange("p (c t) -> p c t", c=8), [pfb[4], pfb[5]], [cur_po.b])
                    if ti % 2 == 1:
                        blk = ti // 2

                        def st_q(blk=blk, stg_=cur_stage, po_=cur_po):
                            self.load(Dm["qT_d"][:, :, blk * 256:(blk + 1) * 256].rearrange("h p t -> p h t"),
                                      stg_.t[:], [stg_.b], [DB["qT_d"]], cname="st")
                            self.load(Dm["mixT_d"][0:8, :, blk * 256:(blk + 1) * 256].rearrange("c p t -> p c t"),
                                      po_.t[:], [po_.b], [DB["mixT_d"]], cname="st")
                        deferred.append(st_q)
                p_prev = p_cur
            else:
                zs = zss.next()
                self.dve_copy(zs.t[:], pf[:, 0:1024], [pfb[0], pfb[1]], [zs.b])
                ktok = rtok.next()
                self.rope_apply(zs, ktok, cs, gt, tmp)
                if gt % 2 == 0:
                    cur_stage = tstage.next()
                for h in range(8):
                    self.tr(pbt[:, h * 128:(h + 1) * 128], ktok.t[:, h * 128:(h + 1) * 128], idb.t[:],
                            [ktok.b, idb.b], [pbb[0]])
                self.act_copy(cur_stage.t[:, :, (gt % 2) * 128:(gt % 2 + 1) * 128],
                              pbt[:, 0:1024].rearrange("p (h t) -> p h t", h=8), [pbb[0]], [cur_stage.b])
                v = vtok.next()
                self.dve_copy(v.t[:], pf[:, 1024:2048], [pfb[2], pfb[3]], [v.b])

                def st_v(gt=gt, v=v):
                    self.load(Dm["v_d"][gt * 128:(gt + 1) * 128, :], v.t[:], [v.b], [DB["v_d"]], cname="st")
                deferred.append(st_v)
                if gt % 2 == 1:
                    blk = gt // 2

                    def st_k(blk=blk, stg_=cur_stage):
                        self.load(Dm["kT_d"][:, :, blk * 256:(blk + 1) * 256].rearrange("h p t -> p h t"),
                                  stg_.t[:], [stg_.b], [DB["kT_d"]], cname="st")
                    deferred.append(st_k)
        for f in deferred:
            f()

    def st_attn(self, es):
        Dm, DB, P = self.D, self.DB, self.P
        (pf, pfb), _ = self.psum(es, 8, 0)
        masks = self.sb(es, "masks", [128, 3, 512], BF16)
        ones = self.sb(es, "ones", [128, 128], BF16)
        self.load(masks.t[:], Dm["masks"][:, :, :], [], [masks.b])
        self.load(ones.t[:], Dm["ones_bf"][:, :], [], [ones.b])
        qTs = self.rot(es, "qTh", [128, NT], BF16, 2)
        kTs = self.rot(es, "kTh", [128, NT + NH], BF16, 2)
        Vs = self.rot(es, "Vh", [128, 69, 128], BF16, 2)
        Oacc = self.sb(es, "Oacc", [128, NT], F32)
        Lacc = self.sb(es, "Lacc", [128, NT], F32)
        Pts = self.rot(es, "Pt", [128, 2, 512], BF16, 3)
        outs = self.rot(es, "attn_o", [128, NT], BF16, 2)
        scale = 128.0 ** -0.5

        def head_loads(h):
            qT, kT, V = qTs.next(), kTs.next(), Vs.next()
            self.load(qT.t[:], Dm["qT_d"][h], [DB["qT_d"]], [qT.b])
            self.load(kT.t[:], Dm["kT_d"][h], [DB["kT_d"]], [kT.b])
            vh = Dm["v_d"][:, h * 128:(h + 1) * 128]
            self.load(V.t[:, 0:17, :], vh[1920:4096, :].rearrange("(j p) c -> p j c", p=128), [DB["v_d"]], [V.b])
            v4 = vh.rearrange("(a d) c -> d a c", d=4)
            for r in range(4):
                self.load(V.t[:, 17 + r * 5: 17 + (r + 1) * 5, :],
                          v4[r, 384:1024, :].rearrange("(m p) c -> p m c", p=128), [DB["v_d"]], [V.b])
            v16 = vh.rearrange("(a d) c -> d a c", d=16)
            for r in range(16):
                self.load(V.t[:, 37 + r * 2: 37 + (r + 1) * 2, :],
                          v16[r, :, :].rearrange("(m p) c -> p m c", p=128), [DB["v_d"]], [V.b])
            return qT, kT, V

        def head_groups(qT, kT, V):
            k1 = kT.t[:]
            k4 = kT.t[:].rearrange("p (a d) -> p d a", d=4)
            k16 = kT.t[:].rearrange("p (a d) -> p d a", d=16)
            q1 = qT.t[:]
            q4 = qT.t[:].rearrange("p (a d) -> p d a", d=4)
            q16 = qT.t[:].rearrange("p (a d) -> p d a", d=16)
            groups = []
            for gi in range(4):
                units = []
                for u in range(4):
                    n = 4 * gi + u
                    units.append((k1[:, 1920 + 128 * n: 2048 + 128 * n], k1[:, 2048 + 128 * n: 2176 + 128 * n],
                                  q1[:, 128 * n:128 * (n + 1)], n, n + 1))
                mv = (2, 0) if gi == 0 else (0, 0)
                groups.append((units, mv, lambda acc, gi=gi: acc[:, 512 * gi:512 * (gi + 1)].rearrange("p (u q) -> p u q", u=4)))
            for n in range(4):
                units = []
                for r in range(4):
                    units.append((k4[:, r, 384 + 128 * n: 512 + 128 * n], k4[:, r, 512 + 128 * n: 640 + 128 * n],
                                  q4[:, r, 128 * n:128 * (n + 1)], 17 + r * 5 + n, 17 + r * 5 + n + 1))
                mv = (1, 1) if n == 0 else (0, 0)
                groups.append((units, mv, lambda acc, n=n: acc[:, 512 * n:512 * (n + 1)].rearrange("p (q r) -> p r q", r=4)))
            for gi in range(4):
                units = []
                for u in range(4):
                    r = 4 * gi + u
                    units.append((k16[:, r, 0:128], k16[:, r, 128:256], q16[:, r, 0:128], 37 + r * 2, 38 + r * 2))
                groups.append((units, (1, 1), lambda acc, gi=gi: acc[:].rearrange("p (q r) -> p r q", r=16)[:, 4 * gi:4 * gi + 4, :]))
            return groups

        items = []
        res = {0: head_loads(0)}

        def s_part(idx):
            h, gi = idx // 12, idx % 12
            if gi == 1 and h + 1 < 8:
                res[h + 1] = head_loads(h + 1)
            qT, kT, V = res[h]
            units, mv, accv = head_groups(qT, kT, V)[gi]
            s_ = idx % 2
            bSA, bSB = 4 * s_, 4 * s_ + 1
            for u, (kp, kc, qq, vp, vc) in enumerate(units):
                bank = bSA if u < 2 else bSB
                base = bank * 512 + (u % 2) * 256
                self.mm(pf[:, base:base + 128], kp, qq, True, True, [kT.b, qT.b], [pfb[bank]])
                self.mm(pf[:, base + 128:base + 256], kc, qq, True, True, [kT.b, qT.b], [pfb[bank]])
            Pt = Pts.next()
            for j, bank in enumerate((bSA, bSB)):
                P.op("act", lambda e, j=j, bank=bank, Pt=Pt: e.activation(
                    out=Pt.t[:, j, :], in_=pf[:, bank * 512:(bank + 1) * 512], func=AF.Exp, scale=scale),
                    reads=[pfb[bank]], writes=[Pt.b])
            for j in range(2):
                self.tt("dve", Pt.t[:, j, :], Pt.t[:, j, :], masks.t[:, mv[j], :], ALU.mult,
                        [Pt.b, masks.b], [Pt.b])
            return (units, accv, V, Pt, s_)

        def o_part(idx, st_):
            h, gi = idx // 12, idx % 12
            units, accv, V, Pt, s_ = st_
            bO, bL = 4 * s_ + 2, 4 * s_ + 3
            if gi == 0:
                P.op("dve", lambda e: e.memset(Oacc.t[:], 0.0), writes=[Oacc.b])
                P.op("dve", lambda e: e.memset(Lacc.t[:], 0.0), writes=[Lacc.b])
            for u, (kp, kc, qq, vp, vc) in enumerate(units):
                j, o = u // 2, (u % 2) * 256
                pp = Pt.t[:, j, o:o + 128]
                pc = Pt.t[:, j, o + 128:o + 256]
                oo = pf[:, bO * 512 + u * 128: bO * 512 + (u + 1) * 128]
                ll = pf[:, bL * 512 + u * 128: bL * 512 + (u + 1) * 128]
                self.mm(oo, V.t[:, vp, :], pp, u == 0, False, [V.b, Pt.b], [pfb[bO]])
                self.mm(oo, V.t[:, vc, :], pc, False, True, [V.b, Pt.b], [pfb[bO]])
                self.mm(ll, ones.t[:], pp, u == 0, False, [ones.b, Pt.b], [pfb[bL]])
                self.mm(ll, ones.t[:], pc, False, True, [ones.b, Pt.b], [pfb[bL]])
            ov = accv(Oacc.t)
            lv = accv(Lacc.t)
            self.tt("dve", ov, ov, pf[:, bO * 512:(bO + 1) * 512].rearrange("p (u q) -> p u q", u=4), ALU.add,
                    [Oacc.b, pfb[bO]], [Oacc.b])
            self.tt("dve", lv, lv, pf[:, bL * 512:(bL + 1) * 512].rearrange("p (u q) -> p u q", u=4), ALU.add,
                    [Lacc.b, pfb[bL]], [Lacc.b])
            if gi == 11:
                o = outs.next()
                P.op("dve", lambda e: e.reciprocal(out=Lacc.t[:], in_=Lacc.t[:]), reads=[Lacc.b], writes=[Lacc.b])
                self.tt("dve", o.t[:], Oacc.t[:], Lacc.t[:], ALU.mult, [Oacc.b, Lacc.b], [o.b])
                self.load(Dm["mixT_d"][8 + h], o.t[:], [o.b], [DB["mixT_d"]], cname="st")
                res.pop(h, None)

        nit = 96
        pend = s_part(0)
        for idx in range(nit):
            nxt = s_part(idx + 1) if idx + 1 < nit else None
            o_part(idx, pend)
            pend = nxt

    def st_mem(self, es):
        Dm, DB, P = self.D, self.DB, self.P
        (pf, pfb), (pbt, pbb) = self.psum(es, 4, 2)
        c = self.consts(es)
        idb = c["idb"]
        ns = self.norm_scratch(es)
        gm = self.sb(es, "gmem", [128, 16], F32)
        self.load(gm.t[:], Dm["g_mem_t"][:, :], [], [gm.b])
        wck = self.sb(es, "wck", [128, 16, 512], BF16)
        wcv = self.sb(es, "wcv", [128, 16, 512], BF16)
        stg = self.rot(es, "wstg", [128, 512], F32, 2)
        self.load_w(wck, lambda cc: Dm["w_ck"][cc * 128:(cc + 1) * 128, :], 16, 512, gm, stg)
        self.load_w(wcv, lambda cc: Dm["w_cv"][cc * 128:(cc + 1) * 128, :], 16, 512, gm, stg)
        memT = self.sb(es, "memT", [128, 16, 256], BF16)
        xts = self.rot(es, "xt", [128, D], F32, 2)
        for t in range(2):
            xt = xts.next()
            self.load(xt.t[:], Dm["mem"][t * 128:(t + 1) * 128, :], [], [xt.b])
            self.norm_T(xt.t[:], xt.b, lambda a, b_, t=t: memT.t[:, a:b_, t * 128:(t + 1) * 128], memT.b, ns, idb,
                        (pbt, pbb), 0, 1)
        kcT = self.sb(es, "kcT", [128, 4, 256], BF16)
        vc = self.sb(es, "vc", [128, 2, 512], BF16)
        for h in range(4):
            bank = h // 2
            o = pf[:, h * 256:(h + 1) * 256]
            for cc in range(16):
                self.mm(o, wck.t[:, cc, h * 128:(h + 1) * 128], memT.t[:, cc, :], (cc == 0 and h % 2 == 0), cc == 15,
                        [wck.b, memT.b], [pfb[bank]])
        self.act_copy(kcT.t[:], pf[:, 0:1024].rearrange("p (h m) -> p h m", h=4), [pfb[0], pfb[1]], [kcT.b])
        for t in range(2):
            o = pf[:, 1024 + t * 512: 1024 + (t + 1) * 512]
            for cc in range(16):
                self.mm(o, memT.t[:, cc, t * 128:(t + 1) * 128], wcv.t[:, cc, :], cc == 0, cc == 15,
                        [wcv.b, memT.b], [pfb[2 + t]])
        self.dve_copy(vc.t[:], pf[:, 1024:2048].rearrange("p (t e) -> p t e", t=2), [pfb[2], pfb[3]], [vc.b])
        self.load(Dm["kcT_d"][:, :, :], kcT.t[:], [kcT.b], [DB["kcT_d"]], cname="st")
        self.load(Dm["vc_d"][:, :, :], vc.t[:], [vc.b], [DB["vc_d"]], cname="st")

    def st_out_cross(self, es):
        Dm, DB, P = self.D, self.DB, self.P
        (pf, pfb), (pbt, pbb) = self.psum(es, 6, 2)
        c = self.consts(es)
        idb = c["idb"]
        ns = self.norm_scratch(es)
        ones = self.sb(es, "ones", [128, 128], BF16)
        self.load(ones.t[:], Dm["ones_bf"][:, :], [], [ones.b])
        gc = self.sb(es, "gcross", [128, 16], F32)
        self.load(gc.t[:], Dm["g_cross_t"][:, :], [], [gc.b])
        stg = self.rot(es, "wstg", [128, 2048], F32, 2)
        Wo = self.sb(es, "Wo", [128, 16, 2048], BF16)
        self.load_w(Wo, lambda cc: Dm["w_out"][cc * 128:(cc + 1) * 128, :], 16, 2048, None, stg)
        wcq = self.sb(es, "wcq", [128, 16, 512], BF16)
        self.load_w(wcq, lambda cc: Dm["w_cq"][cc * 128:(cc + 1) * 128, :], 16, 512, gc, stg)
        wco = self.sb(es, "wco", [128, 4, 2048], BF16)
        self.load_w(wco, lambda cc: Dm["w_co"][cc * 128:(cc + 1) * 128, :], 4, 2048, None, stg)
        kcT = self.sb(es, "kcT", [128, 4, 256], BF16)
        vc = self.sb(es, "vc", [128, 2, 512], BF16)
        self.load(kcT.t[:], Dm["kcT_d"][:, :, :], [DB["kcT_d"]], [kcT.b])
        self.load(vc.t[:], Dm["vc_d"][:, :, :], [DB["vc_d"]], [vc.b])
        TB = 256
        NTL = TB // 128
        mixTs = self.rot(es, "mixTb", [128, 16, TB], BF16, 2)
        x1s = self.rot(es, "x1b", [128, NTL, D], F32, 1)
        h2T = self.sb(es, "h2T", [128, 16, TB], BF16)
        qcT = self.sb(es, "qcT", [128, 4, TB], BF16)
        ocT = self.sb(es, "ocT", [128, 4, TB], BF16)
        PT = self.rot(es, "PTc", [128, 2, TB], BF16, 2)
        rL = self.sb(es, "rL", [128, TB], F32)
        x2s = self.rot(es, "x2t", [128, D], F32, 2)
        scale = 128.0 ** -0.5
        cstb = self.rot(es, "cstb", [128, 2048], BF16, 2)
        for blk in range(NT // TB):
            mT = mixTs.next()
            self.load(mT.t[:], Dm["mixT_d"][:, :, blk * TB:(blk + 1) * TB].rearrange("c p t -> p c t"),
                      [DB["mixT_d"]], [mT.b])
            x1 = x1s.next()
            self.load(x1.t[:], Dm["x_own"][blk * TB:(blk + 1) * TB, :].rearrange("(t p) d -> p t d", p=128),
                      [], [x1.b])
            self.cast_chunks(8, stg, cstb)
            for t in range(NTL):
                for cb in range(4):
                    for cc in range(16):
                        self.mm(pf[:, cb * 512:(cb + 1) * 512], mT.t[:, cc, t * 128:(t + 1) * 128],
                                Wo.t[:, cc, cb * 512:(cb + 1) * 512], cc == 0, cc == 15, [mT.b, Wo.b], [pfb[cb]])
                self.tt("dve", x1.t[:, t, :], x1.t[:, t, :], pf[:, 0:2048], ALU.add,
                        [x1.b, pfb[0], pfb[1], pfb[2], pfb[3]], [x1.b])
                xn_t = self.norm_A(x1.t[:, t, :], x1.b, ns)
                if t > 0:
                    self.norm_B(xn_prev, lambda a, b_, t=t - 1: h2T.t[:, a:b_, t * 128:(t + 1) * 128], h2T.b,
                                idb, (pbt, pbb), 0, 1)
                xn_prev = xn_t
            self.norm_B(xn_prev, lambda a, b_, t=NTL - 1: h2T.t[:, a:b_, t * 128:(t + 1) * 128], h2T.b,
                        idb, (pbt, pbb), 0, 1)
            for h in range(4):
                bank = (h * TB) // 512
                for cc in range(16):
                    self.mm(pf[:, h * TB:(h + 1) * TB], wcq.t[:, cc, h * 128:(h + 1) * 128], h2T.t[:, cc, :],
                            (cc == 0 and (h * TB) % 512 == 0), cc == 15, [wcq.b, h2T.b], [pfb[bank]])
            self.act_copy(qcT.t[:], pf[:, 0:4 * TB].rearrange("p (h t) -> p h t", h=4), [pfb[0], pfb[1]], [qcT.b])
            def c_s(h):
                pt = PT.next()
                bS = 4 + (h % 2)
                for mc in range(2):
                    self.mm(pf[:, bS * 512 + mc * TB: bS * 512 + (mc + 1) * TB], kcT.t[:, h, mc * 128:(mc + 1) * 128],
                            qcT.t[:, h, :], mc == 0, True, [kcT.b, qcT.b], [pfb[bS]])
                P.op("act", lambda e, bS=bS, pt=pt: e.activation(
                    out=pt.t[:].rearrange("p m t -> p (m t)"), in_=pf[:, bS * 512: bS * 512 + 2 * TB], func=AF.Exp,
                    scale=scale), reads=[pfb[bS]], writes=[pt.b])
                return pt

            def c_o(h, pt):
                bO = 2 + (h % 2)
                oo = pf[:, bO * 512: bO * 512 + TB]
                ll = pf[:, bO * 512 + TB: bO * 512 + 2 * TB]
                for mc in range(2):
                    self.mm(oo, vc.t[:, mc, h * 128:(h + 1) * 128], pt.t[:, mc, :], mc == 0, mc == 1,
                            [vc.b, pt.b], [pfb[bO]])
                for mc in range(2):
                    self.mm(ll, ones.t[:], pt.t[:, mc, :], False, mc == 1, [ones.b, pt.b], [pfb[bO]])
                P.op("dve", lambda e, ll=ll: e.reciprocal(out=rL.t[:], in_=ll), reads=[pfb[bO]], writes=[rL.b])
                self.tt("dve", ocT.t[:, h, :], rL.t[:], oo, ALU.mult, [rL.b, pfb[bO]], [ocT.b])

            ptn = c_s(0)
            for h in range(4):
                ptc = ptn
                if h + 1 < 4:
                    ptn = c_s(h + 1)
                c_o(h, ptc)
            for t in range(NTL):
                for cb in range(4):
                    for hh in range(4):
                        self.mm(pf[:, cb * 512:(cb + 1) * 512], ocT.t[:, hh, t * 128:(t + 1) * 128],
                                wco.t[:, hh, cb * 512:(cb + 1) * 512], hh == 0, hh == 3, [ocT.b, wco.b], [pfb[cb]])
                x2 = x2s.next()
                self.tt("dve", x2.t[:], x1.t[:, t, :], pf[:, 0:2048], ALU.add,
                        [x1.b, pfb[0], pfb[1], pfb[2], pfb[3]], [x2.b])
                row = blk * TB + t * 128
                self.load(Dm["x2_d"][row:row + 128, :], x2.t[:], [x2.b], [DB["x2_d"]], cname="st")

    def st_route(self, es):
        Dm, DB, P = self.D, self.DB, self.P
        (pf, pfb), (pbt, pbb) = self.psum(es, 6, 2)
        c = self.consts(es)
        idb = c["idb"]
        idf = self.sb(es, "idf", [128, 128], F32)
        self.load(idf.t[:], Dm["ident_f"][:, :], [], [idf.b])
        io16 = self.sb(es, "io16", [128, 16], F32)
        self.load(io16.t[:], Dm["iota16"][:, :], [], [io16.b])
        ns = self.norm_scratch(es)
        gf = self.sb(es, "gffn", [128, 16], F32)
        self.load(gf.t[:], Dm["g_ffn_t"][:, :], [], [gf.b])
        stg = self.rot(es, "wstg", [128, 2048], F32, 2)
        Wq = self.sb(es, "Wq", [128, 16, 2048], BF16)
        self.load_w(Wq, lambda cc: Dm["w_pq"][cc * 128:(cc + 1) * 128, :], 16, 2048, gf, stg)
        sk = self.sb(es, "sk", [128, 2, 128], BF16)
        skf = self.sb(es, "skf", [128, 2, 128], F32)
        self.load(skf.t[:, 0, :], Dm["skT1"][:, :], [], [skf.b])
        self.load(skf.t[:, 1, :], Dm["skT2"][:, :], [], [skf.b])
        self.dve_copy(sk.t[:], skf.t[:], [skf.b], [sk.b])
        xts = self.rot(es, "xt", [128, D], F32, 2)
        h3Ts = self.rot(es, "h3T", [128, 16, 512], BF16, 1)
        qTs = self.sb(es, "qTs", [128, 16, 512], BF16)
        tmpk = self.sb(es, "tmpk", [128, 16, 128], F32)
        v12 = self.sb(es, "v12", [128, 2, 8, 16], F32)
        i12 = self.sb(es, "i12", [128, 2, 8, 16], U32)
        i12f = self.sb(es, "i12f", [128, 2, 8, 16], F32)
        cand = self.sb(es, "cand", [128, 8, 16, 16], F32)
        eqs = Rot([self.sb(es, "eq", [128, 8, 16, 16], F32), cand])
        sc = self.sb(es, "sc", [128, 8, 16], F32)
        ci = self.sb(es, "ci", [128, 8, 16], U32)
        rr = self.sb(es, "rr", [128, 2, 8, 16], U32)
        rrf = self.sb(es, "rrf", [128, 2, 8, 16], F32)
        ex = self.sb(es, "ex", [128, 8, 16], F32)
        zz = self.sb(es, "zz", [128, 8], F32)
        egs = self.rot(es, "eg", [128, 3, 128], F32, 2)
        routs = self.rot(es, "rout", [128, 3, 128], F32, 2)
        NEG = -1.0e30
        s12s = self.rot(es, "s12", [128, 2, 8, 128], F32, 2)
        state = {}

        def block_prep(blk):
            h3T = h3Ts.next()
            for t in range(4):
                xt = xts.next()
                row = blk * 512 + t * 128
                self.load(xt.t[:], Dm["x2_d"][row:row + 128, :], [DB["x2_d"]], [xt.b])
                self.norm_T(xt.t[:], xt.b, lambda a, b_, t=t, h3T=h3T: h3T.t[:, a:b_, t * 128:(t + 1) * 128], h3T.b,
                            ns, idb, (pbt, pbb), 0, 1)
            self.load(Dm["h3T_d"][:, :, blk * 512:(blk + 1) * 512].rearrange("c p t -> p c t"), h3T.t[:],
                      [h3T.b], [DB["h3T_d"]], cname="st")
            for ch in range(16):
                bank = ch % 4
                for cc in range(16):
                    self.mm(pf[:, bank * 512:(bank + 1) * 512], Wq.t[:, cc, ch * 128:(ch + 1) * 128], h3T.t[:, cc, :],
                            cc == 0, cc == 15, [Wq.b, h3T.b], [pfb[bank]])
                self.act_copy(qTs.t[:, ch, :], pf[:, bank * 512:(bank + 1) * 512], [pfb[bank]], [qTs.b])

        def front(i):
            if i % 4 == 0:
                block_prep(i // 4)
            t = i % 4
            s12 = s12s.next()
            state[i] = s12
            for half in range(2):
                for h in range(8):
                    bank = half * 2 + h // 4
                    o = pf[:, bank * 512 + (h % 4) * 128: bank * 512 + (h % 4 + 1) * 128]
                    self.mm(o, qTs.t[:, 2 * h + half, t * 128:(t + 1) * 128], sk.t[:, half, :], h % 4 == 0, True,
                            [qTs.b, sk.b], [pfb[bank]])
            self.act_copy(s12.t[:].rearrange("p a h k -> p (a h k)"), pf[:, 0:2048],
                          [pfb[0], pfb[1], pfb[2], pfb[3]], [s12.b])

        def back(i):
            tile_i = i
            s12 = state.pop(i)
            chains = [(half, h) for h in range(8) for half in range(2)]
            for (half, h) in chains:
                sv = s12.t[:, half, h, :]
                P.op("dve", lambda e, sv=sv, half=half, h=h: e.max(out=v12.t[:, half, h, 0:8], in_=sv),
                     reads=[s12.b], writes=[v12.b])
            for ci_, (half, h) in enumerate(chains):
                sv = s12.t[:, half, h, :]
                P.op("dve", lambda e, sv=sv, half=half, h=h, ci_=ci_: e.match_replace(
                    out=tmpk.t[:, ci_, 0:128], in_to_replace=v12.t[:, half, h, 0:8], in_values=sv, imm_value=NEG),
                    reads=[s12.b, v12.b], writes=[tmpk.b])
            for ci_, (half, h) in enumerate(chains):
                P.op("dve", lambda e, half=half, h=h, ci_=ci_: e.max(out=v12.t[:, half, h, 8:16], in_=tmpk.t[:, ci_, 0:128]),
                     reads=[tmpk.b], writes=[v12.b])
            for k0 in (0, 8):
                for (half, h) in chains:
                    sv = s12.t[:, half, h, :]
                    P.op("dve", lambda e, sv=sv, half=half, h=h, k0=k0: e.max_index(
                        out=i12.t[:, half, h, k0:k0 + 8], in_max=v12.t[:, half, h, k0:k0 + 8], in_values=sv),
                        reads=[s12.b, v12.b], writes=[i12.b])
            self.tt("dve", cand.t[:], v12.t[:, 0, :, :].unsqueeze(3).to_broadcast([128, 8, 16, 16]),
                    v12.t[:, 1, :, :].unsqueeze(2).to_broadcast([128, 8, 16, 16]), ALU.add, [v12.b], [cand.b])
            cvs = [cand.t[:, h, :, :].rearrange("p a b -> p (a b)") for h in range(8)]
            for h in range(8):
                P.op("dve", lambda e, cv=cvs[h], h=h: e.max(out=sc.t[:, h, 0:8], in_=cv), reads=[cand.b], writes=[sc.b])
            for h in range(8):
                P.op("dve", lambda e, cv=cvs[h], h=h: e.match_replace(
                    out=tmpk.t[:, 2 * h:2 * h + 2, :].rearrange("p a b -> p (a b)"), in_to_replace=sc.t[:, h, 0:8],
                    in_values=cv, imm_value=NEG), reads=[cand.b, sc.b], writes=[tmpk.b])
            for h in range(8):
                P.op("dve", lambda e, h=h: e.max(out=sc.t[:, h, 8:16],
                                                 in_=tmpk.t[:, 2 * h:2 * h + 2, :].rearrange("p a b -> p (a b)")),
                     reads=[tmpk.b], writes=[sc.b])
            for k0 in (0, 8):
                for h in range(8):
                    P.op("dve", lambda e, cv=cvs[h], h=h, k0=k0: e.max_index(
                        out=ci.t[:, h, k0:k0 + 8], in_max=sc.t[:, h, k0:k0 + 8], in_values=cv),
                        reads=[cand.b, sc.b], writes=[ci.b])
            eg = egs.next()
            self.tt("dve", ex.t[:], sc.t[:], sc.t[:, :, 0:1].to_broadcast([128, 8, 16]), ALU.subtract,
                    [sc.b], [ex.b])
            P.op("act", lambda e: e.activation(out=ex.t[:], in_=ex.t[:], func=AF.Exp), reads=[ex.b], writes=[ex.b])
            P.op("dve", lambda e: e.tensor_single_scalar(out=rr.t[:, 0, :, :], in_=ci.t[:], scalar=4,
                                                        op=ALU.logical_shift_right), reads=[ci.b], writes=[rr.b])
            P.op("dve", lambda e: e.tensor_single_scalar(out=rr.t[:, 1, :, :], in_=ci.t[:], scalar=15,
                                                        op=ALU.bitwise_and), reads=[ci.b], writes=[rr.b])
            self.dve_copy(rrf.t[:], rr.t[:], [rr.b], [rrf.b])
            self.dve_copy(i12f.t[:], i12.t[:], [i12.b], [i12f.b])
            for half in range(2):
                eq_ = eqs.next()
                self.tt("dve", eq_.t[:], rrf.t[:, half, :, :].unsqueeze(3).to_broadcast([128, 8, 16, 16]),
                        io16.t[:].unsqueeze(1).unsqueeze(1).to_broadcast([128, 8, 16, 16]), ALU.is_equal,
                        [rrf.b, io16.b], [eq_.b])
            for half in range(2):
                eq_ = eqs.next()
                self.tt("dve", eq_.t[:], eq_.t[:], i12f.t[:, half, :, :].unsqueeze(2).to_broadcast([128, 8, 16, 16]),
                        ALU.mult, [eq_.b, i12f.b], [eq_.b])
            for half in range(2):
                eq_ = eqs.next()
                P.op("dve", lambda e, half=half, eg=eg, eq_=eq_: e.reduce_sum(
                    out=eg.t[:, half, :].rearrange("p (h k) -> p h k", h=8), in_=eq_.t[:], axis=AX.X),
                    reads=[eq_.b], writes=[eg.b])
            P.op("dve", lambda e: e.reduce_sum(out=zz.t[:], in_=ex.t[:], axis=AX.X), reads=[ex.b], writes=[zz.b])
            P.op("dve", lambda e: e.reciprocal(out=zz.t[:], in_=zz.t[:]), reads=[zz.b], writes=[zz.b])
            self.tt("dve", eg.t[:, 2, :].rearrange("p (h k) -> p h k", h=8), ex.t[:],
                    zz.t[:].unsqueeze(2).to_broadcast([128, 8, 16]), ALU.mult, [ex.b, zz.b], [eg.b])
            for j in range(3):
                self.tr(pf[:, 4 * 512 + j * 128: 4 * 512 + (j + 1) * 128], eg.t[:, j, :], idf.t[:],
                        [eg.b, idf.b], [pfb[4]])
            ro = routs.next()
            self.act_copy(ro.t[:].rearrange("p j t -> p (j t)"), pf[:, 2048:2048 + 384], [pfb[4]], [ro.b])
            self.load(Dm["route_d"][tile_i], ro.t[:], [ro.b], [DB["route_d"]], cname="st")

        cstb = self.rot(es, "cstb", [128, 2048], BF16, 2)
        front(0)
        for i in range(16):
            if i + 1 < 16:
                front(i + 1)
            self.cast_chunks(12, stg, cstb)
            back(i)
        self.cast_chunks(256, stg, cstb)

    def st_experts_v1(self, es):
        Dm, DB, P = self.D, self.DB, self.P
        (pf, pfb), _ = self.psum(es, 8, 0)
        io = self.sb(es, "io128", [128, 128], F32)
        self.load(io.t[:], Dm["iota128"][:, :], [], [io.b])
        iobf = self.sb(es, "iobf", [128, 128], BF16)
        self.dve_copy(iobf.t[:], io.t[:], [io.b], [iobf.b])
        gfin = self.sb(es, "gfin", [128, D], F32)
        self.load(gfin.t[:], Dm["g_final"][0, :].partition_broadcast(128), [], [gfin.b])
        GtH = self.rot(es, "GtH", [128, 64, 256], BF16, 2)
        h3Ts = self.rot(es, "h3Tb", [128, 16, 256], BF16, 2)
        routs = self.rot(es, "routb", [128, 2, 3, 128], F32, 2)
        robs = self.rot(es, "robf", [128, 2, 3, 128], BF16, 2)
        oh1 = self.rot(es, "oh1", [128, 8, 64], BF16, 2)
        oh2 = self.rot(es, "oh2", [128, 8, 128], BF16, 2)
        wus = self.rot(es, "wu", [128, 16, 512], BF16, 2)
        wvs = self.rot(es, "wv", [128, 4, 1024], BF16, 3)
        tA = self.rot(es, "tA", [128, 2, 256], BF16, 2)
        x3s = self.rot(es, "x3", [128, 2, D], F32, 2)
        ss = self.rot(es, "f_ss", [128, 1], F32, 2)
        outs = self.rot(es, "outt", [128, D], F32, 1)
        wuT = Dm["wuT_bf"].rearrange("(c p) e -> p c e", p=128)
        blkres = {}

        def block_loads(blk):
            h3T, ro, x3 = h3Ts.next(), routs.next(), x3s.next()
            tok0 = blk * 256
            self.load(h3T.t[:], Dm["h3T_d"][:, :, tok0:tok0 + 256].rearrange("c p t -> p c t"), [DB["h3T_d"]], [h3T.b])
            self.load(ro.t[:], Dm["route_d"][2 * blk:2 * blk + 2].rearrange("a s j t -> s a j t"),
                      [DB["route_d"]], [ro.b])
            self.load(x3.t[:], Dm["x2_d"][tok0:tok0 + 256, :].rearrange("(t p) d -> p t d", p=128), [DB["x2_d"]], [x3.b])
            blkres[blk] = (h3T, ro, x3)

        def g_batches(u):
            blk, hf = u // 2, u % 2
            if hf == 0:
                block_loads(blk)
            h3T, ro, x3 = blkres[blk]
            G = GtH.next()
            rb = robs.next()
            st_ = {"G": G}

            def prep():
                self.ts("dve", rb.t[:, :, 0, :], ro.t[:, :, 0, :], float(-64 * hf), None, ALU.add, None, [ro.b], [rb.b])
                self.dve_copy(rb.t[:, :, 1:3, :], ro.t[:, :, 1:3, :], [ro.b], [rb.b])
            outl = [prep]
            GB = 8
            for bi, t0 in enumerate(range(0, 256, GB)):
                def part1(bi=bi, t0=t0):
                    a_, tt_ = t0 // 128, t0 % 128
                    o1, o2 = oh1.next(), oh2.next()
                    io64 = iobf.t[:, 0:64].unsqueeze(1).to_broadcast([128, GB, 64])
                    io128 = iobf.t[:].unsqueeze(1).to_broadcast([128, GB, 128])
                    e1b = rb.t[:, a_, 0, tt_:tt_ + GB].unsqueeze(2).to_broadcast([128, GB, 64])
                    gb_ = rb.t[:, a_, 2, tt_:tt_ + GB].unsqueeze(2).to_broadcast([128, GB, 64])
                    e2b = rb.t[:, a_, 1, tt_:tt_ + GB].unsqueeze(2).to_broadcast([128, GB, 128])
                    self.tt("dve", o1.t[:], io64, e1b, ALU.is_equal, [iobf.b, rb.b], [o1.b])
                    self.tt("dve", o1.t[:], o1.t[:], gb_, ALU.mult, [o1.b, rb.b], [o1.b])
                    self.tt("dve", o2.t[:], io128, e2b, ALU.is_equal, [iobf.b, rb.b], [o2.b])

                    def part2(bi=bi, t0=t0, o1=o1, o2=o2):
                        bank = bi % 2
                        for j in range(GB):
                            self.mm(pf[:, bank * 512 + j * 64: bank * 512 + (j + 1) * 64], o2.t[:, j, :], o1.t[:, j, :],
                                    True, True, [o1.b, o2.b], [pfb[bank]])
                        self.act_copy(G.t[:, :, t0:t0 + GB].rearrange("p e t -> p t e"),
                                      pf[:, bank * 512:(bank + 1) * 512].rearrange("p (t e) -> p t e", t=GB),
                                      [pfb[bank]], [G.b])
                    return part2
                outl.append(part1)
            return st_, outl

        nun = 16
        pre_wu = {}
        deferred = []

        def issue_wu(u_, g_):
            hf_ = u_ % 2
            wu_ = wus.next()
            e0_ = (hf_ * 64 + 4 * g_) * 128
            self.load(wu_.t[:], wuT[:, :, e0_:e0_ + 512], [DB["wuT_bf"]], [wu_.b])
            pre_wu[(u_, g_)] = wu_

        cur, cur_list = g_batches(0)
        for f in cur_list:
            r_ = f()
            if r_ is not None:
                r_()
        for u in range(nun):
            blk, hf = u // 2, u % 2
            h3T, ro, x3 = blkres[blk]
            G = cur["G"]
            if u + 1 < nun:
                nxt, nxt_list = g_batches(u + 1)
            else:
                nxt, nxt_list = None, []
            for g in range(16):
                if (u, g) not in pre_wu:
                    issue_wu(u, g)
                wu = pre_wu.pop((u, g))
                for pr in range(2):
                    p_ = 2 * g + pr
                    bank = 2 + p_ % 2
                    pend = None
                    if nxt_list:
                        if p_ == 0:
                            nxt_list.pop(0)()
                        pend = nxt_list.pop(0)()
                    for j in range(2):
                        el = pr * 2 + j
                        for cc in range(16):
                            self.mm(pf[:, bank * 512 + j * 256: bank * 512 + (j + 1) * 256],
                                    wu.t[:, cc, el * 128:(el + 1) * 128], h3T.t[:, cc, :],
                                    (cc == 0 and j == 0), cc == 15, [wu.b, h3T.b], [pfb[bank]])
                    ta = tA.next()
                    P.op("act", lambda e, bank=bank, ta=ta: e.activation(
                        out=ta.t[:].rearrange("p j t -> p (j t)"), in_=pf[:, bank * 512:(bank + 1) * 512],
                        func=AF.Gelu_apprx_tanh), reads=[pfb[bank]], writes=[ta.b])
                    e1l = 4 * g + 2 * pr
                    self.tt("dve", G.t[:, e1l:e1l + 2, :], G.t[:, e1l:e1l + 2, :], ta.t[:], ALU.mult,
                            [G.b, ta.b], [G.b])
                    if pend is not None:
                        pend()
            assert not nxt_list
            if u + 1 < nun:
                issue_wu(u + 1, 0)
                issue_wu(u + 1, 1)
            for f in deferred:
                f()
            deferred = []
            for dh in range(2):
                for g in range(16):
                    wv = wvs.next()
                    r0 = (hf * 64 + 4 * g) * 128
                    self.load(wv.t[:], Dm["wv_bf"][r0:r0 + 512, dh * 1024:(dh + 1) * 1024]
                              .rearrange("(a p) d -> p a d", p=128), [DB["wv_bf"]], [wv.b])
                    for el in range(4):
                        e1l = 4 * g + el
                        for tl in range(2):
                            for db in range(2):
                                bank = 4 + tl * 2 + db
                                self.mm(pf[:, bank * 512:(bank + 1) * 512], G.t[:, e1l, tl * 128:(tl + 1) * 128],
                                        wv.t[:, el, db * 512:(db + 1) * 512], e1l == 0, e1l == 63,
                                        [G.b, wv.b], [pfb[bank]])
                for tl in range(2):
                    dst = x3.t[:, tl, dh * 1024:(dh + 1) * 1024]
                    self.tt("dve", dst, dst, pf[:, (4 + tl * 2) * 512:(6 + tl * 2) * 512], ALU.add,
                            [x3.b, pfb[4 + tl * 2], pfb[5 + tl * 2]], [x3.b])
            if hf == 1:
                tok0 = blk * 256
                for tl in range(2):
                    s_ = ss.next()
                    o = outs.next()
                    P.op("act", lambda e, tl=tl, s_=s_, o=o, x3=x3: e.activation(out=o.t[:], in_=x3.t[:, tl, :],
                                                                                 func=AF.Square, accum_out=s_.t[:]),
                         reads=[x3.b], writes=[o.b, s_.b])
                    P.op("act", lambda e, s_=s_: e.activation(out=s_.t[:], in_=s_.t[:], func=AF.Sqrt, scale=1.0 / D,
                                                              bias=self.eps.t[:]),
                         reads=[s_.b, self.eps.b], writes=[s_.b])
                    P.op("dve", lambda e, s_=s_: e.reciprocal(out=s_.t[:], in_=s_.t[:]), reads=[s_.b], writes=[s_.b])
                    P.op("dve", lambda e, tl=tl, s_=s_, o=o, x3=x3: e.scalar_tensor_tensor(
                        out=o.t[:], in0=x3.t[:, tl, :], scalar=s_.t[:, 0:1], in1=gfin.t[:], op0=ALU.mult, op1=ALU.mult),
                        reads=[x3.b, s_.b, gfin.b], writes=[o.b])
                    row = tok0 + tl * 128
                    self.load(Dm["out"][row:row + 128, :], o.t[:], [o.b], [DB["out"]], cname="st", queue="act")
                blkres.pop(blk)
            cur = nxt
        for f in deferred:
            f()


    def st_experts(self, es):
        Dm, DB, P = self.D, self.DB, self.P
        (pf, pfb), _ = self.psum(es, 8, 0)
        TB, NTL, NQ, EQ = 512, 4, 4, 32
        io = self.sb(es, "io128", [128, 128], F32)
        self.load(io.t[:], Dm["iota128"][:, :], [], [io.b])
        iobf = self.sb(es, "iobf", [128, 128], BF16)
        self.dve_copy(iobf.t[:], io.t[:], [io.b], [iobf.b])
        gfin = self.sb(es, "gfin", [128, D], F32)
        self.load(gfin.t[:], Dm["g_final"][0, :].partition_broadcast(128), [], [gfin.b])
        GtQ = self.rot(es, "GtQ", [128, EQ, TB], BF16, 2)
        h3Ts = self.rot(es, "h3Tb", [128, 16, TB], BF16, 1)
        routs = self.rot(es, "routb", [128, NTL, 3, 128], F32, 1)
        robs = self.rot(es, "robf", [128, NTL, 3, 128], BF16, 2)
        GB = 16
        oh1 = self.rot(es, "oh1", [128, GB, EQ], BF16, 2)
        oh2 = self.rot(es, "oh2", [128, GB, 128], BF16, 2)
        wus = self.rot(es, "wu", [128, 16, 512], BF16, 2)
        wvs = self.rot(es, "wv", [128, 4, 512], BF16, 3)
        tA = self.rot(es, "tA", [128, TB], BF16, 2)
        x3s = self.rot(es, "x3", [128, NTL, D], F32, 1)
        ss = self.rot(es, "f_ss", [128, 1], F32, 2)
        outs = self.rot(es, "outt", [128, D], F32, 1)
        wuT = Dm["wuT_bf"].rearrange("(c p) e -> p c e", p=128)
        b_h3T, b_ro, b_x3 = {}, {}, {}

        def load_ro(blk):
            ro = routs.next()
            self.load(ro.t[:], Dm["route_d"][NTL * blk:NTL * (blk + 1)].rearrange("a s j t -> s a j t"),
                      [DB["route_d"]], [ro.b])
            b_ro[blk] = ro

        def load_h3T(blk):
            h3T = h3Ts.next()
            tok0 = blk * TB
            self.load(h3T.t[:], Dm["h3T_d"][:, :, tok0:tok0 + TB].rearrange("c p t -> p c t"), [DB["h3T_d"]], [h3T.b])
            b_h3T[blk] = h3T

        def load_x3(blk):
            x3 = x3s.next()
            tok0 = blk * TB
            self.load(x3.t[:], Dm["x2_d"][tok0:tok0 + TB, :].rearrange("(t p) d -> p t d", p=128), [DB["x2_d"]], [x3.b],
                      queue=("sp" if blk == 0 else "act"))
            b_x3[blk] = x3

        def g_batches(u):
            blk, q = u // NQ, u % NQ
            if q == 0:
                load_ro(blk)
            ro = b_ro[blk]
            G = GtQ.next()
            rb = robs.next()
            st_ = {"G": G}

            def prep():
                self.ts("dve", rb.t[:, :, 0, :], ro.t[:, :, 0, :], float(-EQ * q), None, ALU.add, None, [ro.b], [rb.b])
                self.dve_copy(rb.t[:, :, 1:3, :], ro.t[:, :, 1:3, :], [ro.b], [rb.b])
            outl = [prep]
            for bi, t0 in enumerate(range(0, TB, GB)):
                def part1(bi=bi, t0=t0):
                    a_, tt_ = t0 // 128, t0 % 128
                    o1, o2 = oh1.next(), oh2.next()
                    ioq = iobf.t[:, 0:EQ].unsqueeze(1).to_broadcast([128, GB, EQ])
                    io128 = iobf.t[:].unsqueeze(1).to_broadcast([128, GB, 128])
                    e1b = rb.t[:, a_, 0, tt_:tt_ + GB].unsqueeze(2).to_broadcast([128, GB, EQ])
                    gb_ = rb.t[:, a_, 2, tt_:tt_ + GB].unsqueeze(2).to_broadcast([128, GB, EQ])
                    e2b = rb.t[:, a_, 1, tt_:tt_ + GB].unsqueeze(2).to_broadcast([128, GB, 128])
                    self.tt("dve", o1.t[:], ioq, e1b, ALU.is_equal, [iobf.b, rb.b], [o1.b])
                    self.tt("dve", o1.t[:], o1.t[:], gb_, ALU.mult, [o1.b, rb.b], [o1.b])
                    self.tt("dve", o2.t[:], io128, e2b, ALU.is_equal, [iobf.b, rb.b], [o2.b])

                    def part2(bi=bi, t0=t0, o1=o1, o2=o2):
                        bank = bi % 2
                        for j in range(GB):
                            self.mm(pf[:, bank * 512 + j * EQ: bank * 512 + (j + 1) * EQ], o2.t[:, j, :], o1.t[:, j, :],
                                    True, True, [o1.b, o2.b], [pfb[bank]])
                        self.act_copy(G.t[:, :, t0:t0 + GB].rearrange("p e t -> p t e"),
                                      pf[:, bank * 512:(bank + 1) * 512].rearrange("p (t e) -> p t e", t=GB),
                                      [pfb[bank]], [G.b])
                    return part2
                outl.append(part1)
            return st_, outl

        nun = (NT // TB) * NQ
        NG = EQ // 4
        pre_wu = {}

        def issue_wu(u_, g_):
            q_ = u_ % NQ
            wu_ = wus.next()
            e0_ = (q_ * EQ + 4 * g_) * 128
            self.load(wu_.t[:], wuT[:, :, e0_:e0_ + 512], [DB["wuT_bf"]], [wu_.b])
            pre_wu[(u_, g_)] = wu_

        pre_wv = {}

        def issue_wv(u_, dq_, g_):
            q_ = u_ % NQ
            wv_ = wvs.next()
            r0_ = (q_ * EQ + 4 * g_) * 128
            self.load(wv_.t[:], Dm["wv_bf"][r0_:r0_ + 512, dq_ * 512:(dq_ + 1) * 512]
                      .rearrange("(a p) d -> p a d", p=128), [DB["wv_bf"]], [wv_.b])
            pre_wv[(u_, dq_, g_)] = wv_

        fin_list = []
        load_h3T(0)
        load_x3(0)
        cur, cur_list = g_batches(0)
        for f in cur_list:
            r_ = f()
            if r_ is not None:
                r_()
        for u in range(nun):
            blk, q = u // NQ, u % NQ
            h3T = b_h3T[blk]
            G = cur["G"]
            if u + 1 < nun:
                nxt, nxt_list = g_batches(u + 1)
            else:
                nxt, nxt_list = None, []
            for g in range(NG):
                if (u, g) not in pre_wu:
                    issue_wu(u, g)
                wu = pre_wu.pop((u, g))
                if g == 1:
                    for g2 in range(3):
                        issue_wv(u, 0, g2)
                for el in range(4):
                    e1l = 4 * g + el
                    bank = 2 + e1l % 2
                    pend = None
                    if nxt_list:
                        if e1l == 0:
                            nxt_list.pop(0)()
                        pend = nxt_list.pop(0)()
                    for cc in range(16):
                        self.mm(pf[:, bank * 512:(bank + 1) * 512], wu.t[:, cc, el * 128:(el + 1) * 128], h3T.t[:, cc, :],
                                cc == 0, cc == 15, [wu.b, h3T.b], [pfb[bank]])
                    ta = tA.next()
                    P.op("act", lambda e, bank=bank, ta=ta: e.activation(
                        out=ta.t[:], in_=pf[:, bank * 512:(bank + 1) * 512], func=AF.Gelu_apprx_tanh),
                        reads=[pfb[bank]], writes=[ta.b])
                    self.tt("dve", G.t[:, e1l, :], G.t[:, e1l, :], ta.t[:], ALU.mult, [G.b, ta.b], [G.b])
                    if pend is not None:
                        pend()
                    if fin_list and e1l % 4 == 2:
                        fin_list.pop(0)()
            assert not nxt_list
            while fin_list:
                fin_list.pop(0)()
            x3 = b_x3[blk]
            if u + 1 < nun:
                issue_wu(u + 1, 0)
                issue_wu(u + 1, 1)
                if q == NQ - 1:
                    load_h3T(blk + 1)
            for dq in range(4):
                for g in range(NG):
                    if (u, dq, g) not in pre_wv:
                        issue_wv(u, dq, g)
                    wv = pre_wv.pop((u, dq, g))
                    for el in range(4):
                        e1l = 4 * g + el
                        for tl in range(NTL):
                            bank = 4 + tl
                            self.mm(pf[:, bank * 512:(bank + 1) * 512], G.t[:, e1l, tl * 128:(tl + 1) * 128],
                                    wv.t[:, el, :], e1l == 0, e1l == EQ - 1, [G.b, wv.b], [pfb[bank]])
                for tl in range(NTL):
                    dst = x3.t[:, tl, dq * 512:(dq + 1) * 512]
                    self.tt("dve", dst, dst, pf[:, (4 + tl) * 512:(5 + tl) * 512], ALU.add,
                            [x3.b, pfb[4 + tl]], [x3.b])
            if q == NQ - 1:
                tok0 = blk * TB

                def fin(tl, x3=x3, tok0=tok0):
                    s_ = ss.next()
                    o = outs.next()
                    P.op("act", lambda e, tl=tl, s_=s_, o=o, x3=x3: e.activation(out=o.t[:], in_=x3.t[:, tl, :],
                                                                                 func=AF.Square, accum_out=s_.t[:]),
                         reads=[x3.b], writes=[o.b, s_.b])
                    P.op("act", lambda e, s_=s_: e.activation(out=s_.t[:], in_=s_.t[:], func=AF.Sqrt, scale=1.0 / D,
                                                              bias=self.eps.t[:]),
                         reads=[s_.b, self.eps.b], writes=[s_.b])
                    P.op("dve", lambda e, s_=s_: e.reciprocal(out=s_.t[:], in_=s_.t[:]), reads=[s_.b], writes=[s_.b])
                    P.op("dve", lambda e, tl=tl, s_=s_, o=o, x3=x3: e.scalar_tensor_tensor(
                        out=o.t[:], in0=x3.t[:, tl, :], scalar=s_.t[:, 0:1], in1=gfin.t[:], op0=ALU.mult, op1=ALU.mult),
                        reads=[x3.b, s_.b, gfin.b], writes=[o.b])
                    row = tok0 + tl * 128
                    self.load(Dm["out"][row:row + 128, :], o.t[:], [o.b], [DB["out"]], cname="st", queue="act")
                for tl in range(NTL):
                    fin_list.append(lambda tl=tl, fin=fin: fin(tl))
                if u + 1 < nun:
                    fin_list.append(lambda blk=blk: load_x3(blk + 1))
            cur = nxt
        while fin_list:
            fin_list.pop(0)()


def _host_consts(half):
    bf = ml_dtypes.bfloat16
    c = {}
    c["ident_bf"] = np.eye(128, dtype=np.float32).astype(bf)
    c["ident_f"] = np.eye(128, dtype=np.float32)
    c["ones_bf"] = np.ones((128, 128), np.float32).astype(bf)
    kk = np.arange(128)[:, None]
    qq = np.arange(128)[None, :]
    cur = (kk <= qq).astype(np.float32)
    prev = (kk >= qq).astype(np.float32)
    prevh = prev * float(half)
    m = np.zeros((128, 3, 512), np.float32)
    m[:, 0] = np.concatenate([prev, cur, prev, cur], axis=1)
    m[:, 1] = np.concatenate([prevh, cur, prevh, cur], axis=1)
    m[:, 2] = np.concatenate([prevh, cur, prev, cur], axis=1)
    c["masks"] = m.astype(bf)
    A = np.zeros((128, 4, 3, 128), np.float32)
    s = np.arange(128)[:, None]
    t = np.arange(128)[None, :]
    for g, w in enumerate((2, 4, 8, 16)):
        diag = ((s <= t) & (s > t - w)).astype(np.float32) / w
        off = ((s + 128 - 128 <= 127) & (s - 128 > t - w)).astype(np.float32) / w
        first = diag.copy()
        if half == 0:
            cnt = np.minimum(t + 1, w).astype(np.float32)
            first = ((s <= t) & (s > t - w)).astype(np.float32) / cnt
        A[:, g, 0, :] = first - np.eye(128, dtype=np.float32)
        A[:, g, 1, :] = diag - np.eye(128, dtype=np.float32)
        A[:, g, 2, :] = off
    c["poolA"] = A.astype(bf)
    inv = (500000.0 ** (-np.arange(0, 32, 2, dtype=np.float32) / 32)).astype(np.float32)
    rc = np.zeros((128, 2, 32), np.float32)
    rc[:, 0, :] = np.concatenate([inv, inv])[None, :]
    rc[:, 1, 16:] = np.float32(np.pi / 2)
    c["rope_c"] = rc
    c["iota128"] = np.tile(np.arange(128, dtype=np.float32)[None, :], (128, 1))
    c["iota16"] = np.tile(np.arange(16, dtype=np.float32)[None, :], (128, 1))
    c["eps_c"] = np.full((128, 1), 1e-6, np.float32)
    return c


def make_in_maps(inp, cores=range(NCORES)):
    f = lambda a: np.ascontiguousarray(np.asarray(a))
    x = f(inp["x"]); mem = f(inp["mem"]); pos = f(inp["positions"]).astype(np.int32)
    shared = {
        "w_in": f(inp["w_in"][0]), "w_pool": f(inp["w_pool"][0]), "pool_scale": f(inp["pool_scale"][0]),
        "w_out": f(inp["w_out"][0]), "w_cq": f(inp["w_cq"][0]), "w_ck": f(inp["w_ck"][0]),
        "w_cv": f(inp["w_cv"][0]), "w_co": f(inp["w_co"][0]), "w_pq": f(inp["w_pq"][0]),
        "skT1": f(np.asarray(inp["sub_keys_1"][0]).T), "skT2": f(np.asarray(inp["sub_keys_2"][0]).T),
        "w_uT": f(np.asarray(inp["w_u"][0]).T), "w_v": f(inp["w_v"][0]),
        "g_final": f(np.asarray(inp["g_final"]).reshape(1, D)),
    }
    for k, src in (("g_mix_t", "g_mix"), ("g_cross_t", "g_cross"), ("g_mem_t", "g_mem"), ("g_ffn_t", "g_ffn")):
        shared[k] = f(np.asarray(inp[src][0]).reshape(16, 128).T)
    consts = [_host_consts(0), _host_consts(1)]
    maps = []
    for c in cores:
        b, half = c // 2, c % 2
        m = dict(shared)
        m.update(consts[half])
        s0 = half * NT
        m["x_own"] = f(x[b, s0:s0 + NT])
        if half == 0:
            m["x_halo"] = np.zeros((NH, D), np.float32)
            ph = np.zeros((NH,), np.int32)
        else:
            m["x_halo"] = f(x[b, s0 - NH:s0])
            ph = pos[b, s0 - NH:s0]
        pa = np.concatenate([ph, pos[b, s0:s0 + NT]])
        m["pos_t"] = f(pa.reshape(32, 128).T)
        m["mem"] = f(mem[b])
        maps.append(m)
    return maps


_NC_CACHE = {}


def kernel(**inputs):
    if "nc" not in _NC_CACHE:
        _NC_CACHE["nc"] = KB().build()
    nc = _NC_CACHE["nc"]
    maps = make_in_maps(inputs)
    res = run_bass_kernel_spmd(nc, maps, core_ids=list(range(NCORES)))
    out = np.empty((4, 4096, D), np.float32)
    for c in range(NCORES):
        b, half = c // 2, c % 2
        out[b, half * NT:(half + 1) * NT] = np.asarray(res.results[c]["out"], dtype=np.float32)
    return out
```

```python
from contextlib import ExitStack

import numpy as np
import ml_dtypes
import concourse.bass as bass
import concourse.mybir as mybir
from concourse.bass_utils import run_bass_kernel_spmd

F32 = mybir.dt.float32
BF16 = mybir.dt.bfloat16
I32 = mybir.dt.int32
U32 = mybir.dt.uint32
ALU = mybir.AluOpType
AF = mybir.ActivationFunctionType
AX = mybir.AxisListType

NCORES = 8
D = 2048
NT = 2048
NH = 2048
MAGIC = 12582912.0
TWO_PI = 6.283185307179586
PI_SAFE = 3.1415925


class Buf:
    __slots__ = ("name", "w", "r", "psum")

    def __init__(self, name="", psum=False):
        self.name = name
        self.w = {}
        self.r = {}
        self.psum = psum


class Counter:
    def __init__(self, name, step):
        self.name = name
        self.step = step
        self.val = 0
        self.sem = None


class Stream:
    def __init__(self, name):
        self.name = name
        self.ops = []
        self.seen = {}


class Prog:
    def __init__(self, nc, es):
        self.nc = nc
        self.streams = {n: Stream(n) for n in ("pe", "act", "dve", "pool", "sp")}
        self.counters = {}
        for n in ("pe", "act", "dve", "pool"):
            self._counter(n, 1, es)
        self.groups = {}
        for n, k in (("ld", 8), ("ld2", 4), ("st", 8), ("cld", 4), ("cst", 4)):
            self.groups[n] = [[self._counter(f"{n}{i}", 16, es) for i in range(k)], 0]

    def _counter(self, name, step, es):
        c = Counter(name, step)
        c.sem = es.enter_context(self.nc.semaphore("s_" + name))
        self.counters[name] = c
        return c

    def _emit(self, stream, counter, fn, reads, writes, pe_accum=False, pre=None):
        st = self.streams[stream]
        need = {}

        def req(c, v):
            if need.get(c, 0) < v:
                need[c] = v
        if pre is not None and pre[1] > 0:
            req(*pre)
        for b in reads:
            for c, v in b.w.items():
                req(c, v)
            if b.psum:
                for c, v in b.r.items():
                    if c is not counter:
                        req(c, v)
        for b in writes:
            for c, v in b.w.items():
                if pe_accum and c is counter:
                    continue
                if counter.step == 16 and c.step == 16:
                    continue
                req(c, v)
            for c, v in b.r.items():
                req(c, v)
        waits = []
        for c, v in need.items():
            if st.seen.get(c, 0) < v:
                st.seen[c] = v
                waits.append((c, v))
        counter.val += counter.step
        myv = counter.val
        st.ops.append((waits, fn, counter))
        for b in reads:
            if b.r.get(counter, 0) < myv:
                b.r[counter] = myv
        for b in writes:
            if counter.step == 16 and b.w and all(c.step == 16 for c in b.w):
                b.w[counter] = myv
            else:
                b.w = {counter: myv}
            b.r = {}
        return myv

    def op(self, eng, fn, reads=(), writes=(), pe_accum=False):
        return self._emit(eng, self.counters[eng], fn, reads, writes, pe_accum)

    def dma(self, queue, gname, fn, reads=(), writes=()):
        grp = self.groups[gname]
        c = grp[0][grp[1] % len(grp[0])]
        grp[1] += 1
        return self._emit(queue, c, fn, reads, writes, pre=(c, c.val))

    def barrier(self, exclude=()):
        for st in self.streams.values():
            waits = []
            for c in self.counters.values():
                if c.name.rstrip("0123456789") in exclude or c.val == 0:
                    continue
                if st.seen.get(c, 0) < c.val:
                    st.seen[c] = c.val
                    waits.append((c, c.val))
            st.ops.append((waits, None, None))

    def flush(self, es):
        block = es.enter_context(self.nc.Block())
        engs = {"pe": block.tensor, "act": block.scalar, "dve": block.vector,
                "pool": block.gpsimd, "sp": block.sync}
        for name, deco in engs.items():
            st = self.streams[name]
            ops = st.ops
            st.ops = []

            def body(e, ops=ops):
                for waits, fn, counter in ops:
                    for c, v in waits:
                        e.wait_ge(c.sem, v)
                    if fn is not None:
                        fn(e).then_inc(counter.sem, counter.step)
            deco(body)


class Tl:
    __slots__ = ("t", "b")

    def __init__(self, t, b):
        self.t = t
        self.b = b


class Rot:
    def __init__(self, items):
        self.items = items
        self.i = 0

    def next(self):
        x = self.items[self.i % len(self.items)]
        self.i += 1
        return x


class KB:
    def __init__(self, debug=False, upto=99):
        self.debug = debug
        self.upto = upto
        self.nc = bass.Bass("TRN2", target_bir_lowering=False)
        self.D = {}
        self.DB = {}

    def din(self, name, shape, dt):
        self.D[name] = self.nc.dram_tensor(name, list(shape), dt, kind="ExternalInput").ap()
        self.DB[name] = Buf(name)

    def dscr(self, name, shape, dt):
        kind = "ExternalOutput" if self.debug else "Internal"
        self.D[name] = self.nc.dram_tensor(name, list(shape), dt, kind=kind).ap()
        self.DB[name] = Buf(name)

    def dout(self, name, shape, dt):
        self.D[name] = self.nc.dram_tensor(name, list(shape), dt, kind="ExternalOutput").ap()
        self.DB[name] = Buf(name)

    def sb(self, es, name, shape, dt):
        self.uid = getattr(self, "uid", 0) + 1
        name = f"{name}_{self.uid}"
        t = es.enter_context(self.nc.sbuf_tensor(name, list(shape), dt))
        return Tl(t, Buf(name))

    def rot(self, es, name, shape, dt, n):
        return Rot([self.sb(es, f"{name}{i}", shape, dt) for i in range(n)])

    def psum(self, es, nf, nb):
        banks_f, banks_b = [], []
        self.uid = getattr(self, "uid", 0) + 1
        if nf:
            t = es.enter_context(self.nc.psum_tensor(f"psf_{self.uid}", [128, nf * 512], F32))
            bufs = [Buf(f"psf{i}", psum=True) for i in range(nf)]
            banks_f = (t, bufs)
        if nb:
            t2 = es.enter_context(self.nc.psum_tensor(f"psb_{self.uid}", [128, nb * 1024], BF16))
            bufs2 = [Buf(f"psb{i}", psum=True) for i in range(nb)]
            banks_b = (t2, bufs2)
        return banks_f, banks_b

    def mm(self, out, lhsT, rhs, start, stop, R, W):
        self.P.op("pe", lambda e: e.matmul(out, lhsT=lhsT, rhs=rhs, start=start, stop=stop,
                                          skip_group_check=True),
                  reads=R, writes=W, pe_accum=True)

    def tr(self, out, in_, ident, R, W):
        self.P.op("pe", lambda e: e.transpose(out=out, in_=in_, identity=ident),
                  reads=R, writes=W, pe_accum=True)

    def load(self, out, in_, R, W, cname="ld", queue="sp"):
        self.P.dma(queue, cname, lambda e: e.dma_start(out=out, in_=in_), reads=R, writes=W)

    def act_copy(self, out, in_, R, W, scale=None):
        if scale is None:
            self.P.op("act", lambda e: e.copy(out=out, in_=in_), reads=R, writes=W)
        else:
            self.P.op("act", lambda e: e.activation(out=out, in_=in_, func=AF.Copy, scale=scale),
                      reads=R, writes=W)

    def dve_copy(self, out, in_, R, W):
        self.P.op("dve", lambda e: e.tensor_copy(out=out, in_=in_), reads=R, writes=W)

    def tt(self, eng, out, in0, in1, op, R, W):
        self.P.op(eng, lambda e: e.tensor_tensor(out=out, in0=in0, in1=in1, op=op), reads=R, writes=W)

    def ts(self, eng, out, in0, s1, s2, op0, op1, R, W):
        if op1 is None:
            self.P.op(eng, lambda e: e.tensor_scalar(out=out, in0=in0, scalar1=s1, scalar2=None, op0=op0),
                      reads=R, writes=W)
        else:
            self.P.op(eng, lambda e: e.tensor_scalar(out=out, in0=in0, scalar1=s1, scalar2=s2, op0=op0, op1=op1),
                      reads=R, writes=W)

    def consts(self, es):
        c = {}
        c["idb"] = self.sb(es, "idb", [128, 128], BF16)
        self.load(c["idb"].t[:], self.D["ident_bf"][:, :], [self.DB["ident_bf"]], [c["idb"].b])
        return c

    def norm_scratch(self, es):
        s = {}
        s["junk"] = self.sb(es, "n_junk", [128, D], BF16)
        s["ss"] = self.rot(es, "n_ss", [128, 1], F32, 2)
        s["rs"] = self.rot(es, "n_rs", [128, 1], F32, 2)
        s["xn"] = self.rot(es, "n_xn", [128, D], BF16, 2)
        return s

    def norm_A(self, xt_ap, xt_b, ns):
        P = self.P
        ss = ns["ss"].next()
        rs = ns["rs"].next()
        xn = ns["xn"].next()
        junk = ns["junk"]
        P.op("act", lambda e: e.activation(out=junk.t[:], in_=xt_ap, func=AF.Square, accum_out=ss.t[:]),
             reads=[xt_b], writes=[junk.b, ss.b])
        P.op("act", lambda e: e.activation(out=rs.t[:], in_=ss.t[:], func=AF.Sqrt, scale=1.0 / D, bias=self.eps.t[:]),
             reads=[ss.b, self.eps.b], writes=[rs.b])
        P.op("dve", lambda e: e.reciprocal(out=rs.t[:], in_=rs.t[:]), reads=[rs.b], writes=[rs.b])
        self.ts("dve", xn.t[:], xt_ap, rs.t[:, 0:1], None, ALU.mult, None, [xt_b, rs.b], [xn.b])
        return xn

    def norm_B(self, xn, dst_fn, dst_b, idb, psb, bankA, bankB):
        pt, pbufs = psb
        for half, bank in ((0, bankA), (1, bankB)):
            for c in range(8):
                cc = half * 8 + c
                self.tr(pt[:, bank * 1024 + c * 128: bank * 1024 + (c + 1) * 128],
                        xn.t[:, cc * 128:(cc + 1) * 128], idb.t[:], [xn.b, idb.b], [pbufs[bank]])
            src = pt[:, bank * 1024:(bank + 1) * 1024].rearrange("p (c t) -> p c t", c=8)
            if half == 0:
                self.act_copy(dst_fn(0, 8), src, [pbufs[bank]], [dst_b])
            else:
                self.dve_copy(dst_fn(8, 16), src, [pbufs[bank]], [dst_b])

    def norm_T(self, xt_ap, xt_b, dst_fn, dst_b, ns, idb, psb, bankA, bankB):
        xn = self.norm_A(xt_ap, xt_b, ns)
        self.norm_B(xn, dst_fn, dst_b, idb, psb, bankA, bankB)

    def load_w(self, dst, src_fn, nck, ncols, g, stg):
        for c in range(nck):
            s = stg.next()
            self.load(s.t[:, 0:ncols], src_fn(c), [], [s.b], cname="ld2")
            if g is None:
                if c % 2 == 0:
                    self.act_copy(dst.t[:, c, :], s.t[:, 0:ncols], [s.b], [dst.b])
                else:
                    self.dve_copy(dst.t[:, c, :], s.t[:, 0:ncols], [s.b], [dst.b])
            else:
                if c % 2 == 0:
                    self.act_copy(dst.t[:, c, :], s.t[:, 0:ncols], [s.b, g.b], [dst.b], scale=g.t[:, c:c + 1])
                else:
                    self.ts("dve", dst.t[:, c, :], s.t[:, 0:ncols], g.t[:, c:c + 1], None, ALU.mult, None,
                            [s.b, g.b], [dst.b])

    def end_stage(self, es):
        self.P.barrier()
        self.P.flush(es)

    def declare(self):
        di, ds = self.din, self.dscr
        di("x_own", [NT, D], F32); di("x_halo", [NH, D], F32); di("mem", [256, D], F32)
        di("pos_t", [128, 32], I32)
        di("w_in", [D, 4096], F32); di("w_pool", [4, 256, 256], F32); di("pool_scale", [4, 256], F32)
        di("w_out", [D, D], F32)
        di("w_cq", [D, 512], F32); di("w_ck", [D, 512], F32); di("w_cv", [D, 512], F32); di("w_co", [512, D], F32)
        di("w_pq", [D, D], F32); di("skT1", [128, 128], F32); di("skT2", [128, 128], F32)
        di("w_uT", [D, 16384], F32); di("w_v", [16384, D], F32)
        for g in ("g_mix_t", "g_cross_t", "g_mem_t", "g_ffn_t"):
            di(g, [128, 16], F32)
        di("g_final", [1, D], F32)
        di("ident_bf", [128, 128], BF16); di("ident_f", [128, 128], F32); di("ones_bf", [128, 128], BF16)
        di("masks", [128, 3, 512], BF16); di("poolA", [128, 4, 3, 128], BF16)
        di("rope_c", [128, 2, 32], F32); di("iota128", [128, 128], F32); di("iota16", [128, 16], F32)
        di("eps_c", [128, 1], F32)
        ds("qT_d", [8, 128, NT], BF16); ds("kT_d", [8, 128, NT + NH], BF16); ds("v_d", [NT + NH, 1024], BF16)
        ds("mixT_d", [16, 128, NT], BF16)
        ds("kcT_d", [128, 4, 256], BF16); ds("vc_d", [128, 2, 512], BF16)
        ds("x2_d", [NT, D], F32); ds("h3T_d", [16, 128, NT], BF16); ds("route_d", [16, 128, 3, 128], F32)
        ds("wuT_bf", [D, 16384], BF16); ds("wv_bf", [16384, D], BF16)
        self.dout("out", [NT, D], F32)

    def build(self):
        nc = self.nc
        self.declare()
        self.cast_pos = 0
        with ExitStack() as top:
            self.P = Prog(nc, top)
            stages = [self.st_prep, self.st_inproj_pq, self.st_inproj_kv, self.st_attn, self.st_mem,
                      self.st_out_cross, self.st_route, self.st_experts]
            for i, st in enumerate(stages):
                if i > self.upto:
                    break
                with ExitStack() as es:
                    self.eps = self.sb(es, "eps", [128, 1], F32)
                    self.load(self.eps.t[:], self.D["eps_c"][:, :], [], [self.eps.b])
                    st(es)
                    self.end_stage(es)
            with ExitStack() as es:
                self.P.barrier()
                self.P.flush(es)
        return nc

    def st_prep(self, es):
        pass

    def cast_chunks(self, n, stg, cstb):
        Dm, DB = self.D, self.DB
        src_u = Dm["w_uT"].rearrange("r (a c) -> (r a) c", c=2048)
        dst_u = Dm["wuT_bf"].rearrange("r (a c) -> (r a) c", c=2048)
        for _ in range(n):
            k = self.cast_pos
            if k >= 256:
                return
            self.cast_pos += 1
            if k < 128:
                src, dst, nm = src_u[k * 128:(k + 1) * 128, :], dst_u[k * 128:(k + 1) * 128, :], "wuT_bf"
            else:
                k2 = k - 128
                src, dst, nm = Dm["w_v"][k2 * 128:(k2 + 1) * 128, :], Dm["wv_bf"][k2 * 128:(k2 + 1) * 128, :], "wv_bf"
            s_ = stg.next()
            self.load(s_.t[:, 0:2048], src, [], [s_.b], cname="cld")
            o = cstb.next()
            self.act_copy(o.t[:], s_.t[:, 0:2048], [s_.b], [o.b])
            self.load(dst, o.t[:], [o.b], [DB[nm]], cname="cst", queue="act")

    def rope_tables(self, es):
        Dm, DB = self.D, self.DB
        posi = self.sb(es, "posi", [128, 32], I32)
        posf = self.sb(es, "posf", [128, 32], F32)
        rc = self.sb(es, "rc", [128, 2, 32], F32)
        ang = self.sb(es, "ang", [128, 32, 32], F32)
        kk = self.sb(es, "kk", [128, 32, 32], F32)
        cs = self.sb(es, "cs", [128, 32, 32], F32)
        self.load(posi.t[:], Dm["pos_t"][:, :], [], [posi.b])
        self.load(rc.t[:], Dm["rope_c"][:, :, :], [], [rc.b])
        self.dve_copy(posf.t[:], posi.t[:], [posi.b], [posf.b])
        pb = posf.t[:].unsqueeze(2).to_broadcast([128, 32, 32])
        invb = rc.t[:, 0, :].unsqueeze(1).to_broadcast([128, 32, 32])
        phb = rc.t[:, 1, :].unsqueeze(1).to_broadcast([128, 32, 32])
        self.tt("dve", ang.t[:], pb, invb, ALU.mult, [posf.b, rc.b], [ang.b])
        self.tt("dve", ang.t[:], ang.t[:], phb, ALU.add, [ang.b, rc.b], [ang.b])
        self.ts("dve", kk.t[:], ang.t[:], 1.0 / TWO_PI, MAGIC, ALU.mult, ALU.add, [ang.b], [kk.b])
        self.ts("dve", kk.t[:], kk.t[:], MAGIC, None, ALU.subtract, None, [kk.b], [kk.b])
        self.P.op("dve", lambda e: e.scalar_tensor_tensor(out=ang.t[:], in0=kk.t[:], scalar=-TWO_PI, in1=ang.t[:],
                                                          op0=ALU.mult, op1=ALU.add),
                  reads=[kk.b, ang.b], writes=[ang.b])
        self.ts("dve", ang.t[:], ang.t[:], -PI_SAFE, PI_SAFE, ALU.max, ALU.min, [ang.b], [ang.b])
        self.P.op("act", lambda e: e.activation(out=cs.t[:], in_=ang.t[:], func=AF.Sin), reads=[ang.b], writes=[cs.b])
        return cs

    def rope_apply(self, zs, out, cs, gt, tmp):
        z3 = zs.t[:].rearrange("p (h d) -> p h d", h=8)
        o3 = out.t[:].rearrange("p (h d) -> p h d", h=8)
        sinb = cs.t[:, gt, 0:16].unsqueeze(1).to_broadcast([128, 8, 16])
        cosb = cs.t[:, gt, 16:32].unsqueeze(1).to_broadcast([128, 8, 16])
        x1 = z3[:, :, 0:16]
        x2 = z3[:, :, 16:32]
        t1, t2 = tmp.next(), tmp.next()
        self.tt("dve", t1.t[:], x1, cosb, ALU.mult, [zs.b, cs.b], [t1.b])
        self.tt("dve", t2.t[:], x2, sinb, ALU.mult, [zs.b, cs.b], [t2.b])
        self.tt("dve", o3[:, :, 0:16], t1.t[:], t2.t[:], ALU.subtract, [t1.b, t2.b], [out.b])
        t3, t4 = tmp.next(), tmp.next()
        self.tt("dve", t3.t[:], x2, cosb, ALU.mult, [zs.b, cs.b], [t3.b])
        self.tt("dve", t4.t[:], x1, sinb, ALU.mult, [zs.b, cs.b], [t4.b])
        self.tt("dve", o3[:, :, 16:32], t3.t[:], t4.t[:], ALU.add, [t3.b, t4.b], [out.b])
        self.dve_copy(o3[:, :, 32:128], z3[:, :, 32:128], [zs.b], [out.b])

    def st_inproj_pq(self, es):
        self.inproj(es, "pq")

    def st_inproj_kv(self, es):
        self.inproj(es, "kv")

    def inproj(self, es, mode):
        Dm, DB, P = self.D, self.DB, self.P
        col0 = 0 if mode == "pq" else 2048
        (pf, pfb), (pbt, pbb) = self.psum(es, 6, 2)
        psb = (pbt, pbb)
        c = self.consts(es)
        idb = c["idb"]
        ns = self.norm_scratch(es)
        cs = self.rope_tables(es)
        gt_ = self.sb(es, "gmix", [128, 16], F32)
        self.load(gt_.t[:], Dm["g_mix_t"][:, :], [], [gt_.b])
        W = self.sb(es, "W", [128, 16, 2048], BF16)
        stg = self.rot(es, "wstg", [128, 512], F32, 8)
        k_ = 0
        for cb in range(4):
            for cc in range(16):
                s_ = stg.next()
                self.load(s_.t[:], Dm["w_in"][cc * 128:(cc + 1) * 128, col0 + cb * 512: col0 + (cb + 1) * 512],
                          [], [s_.b], cname="ld2")
                dst = W.t[:, cc, cb * 512:(cb + 1) * 512]
                if k_ % 2 == 0:
                    self.act_copy(dst, s_.t[:], [s_.b, gt_.b], [W.b], scale=gt_.t[:, cc:cc + 1])
                else:
                    self.ts("dve", dst, s_.t[:], gt_.t[:, cc:cc + 1], None, ALU.mult, None, [s_.b, gt_.b], [W.b])
                k_ += 1
        xts = self.rot(es, "xt", [128, D], F32, 3)
        xnTs = self.rot(es, "xnT", [128, 16, 128], BF16, 2)
        zss = self.rot(es, "zs", [128, 1024], F32, 2)
        rtok = self.rot(es, "rtok", [128, 1024], BF16, 2)
        tmp = self.rot(es, "rtmp", [128, 8, 16], F32, 4)
        tstage = self.rot(es, "tstage", [128, 8, 256], BF16, 2)
        if mode == "pq":
            ptok = self.rot(es, "ptok", [128, 1024], BF16, 3)
            A = self.sb(es, "poolA", [128, 4, 3, 128], BF16)
            self.load(A.t[:], Dm["poolA"][:, :, :, :], [], [A.b])
            wp = self.sb(es, "wp", [128, 4, 2, 256], BF16)
            wpf = self.sb(es, "wpf", [128, 4, 2, 256], F32)
            scb = self.sb(es, "scb", [128, 4, 256], F32)
            self.load(wpf.t[:].rearrange("p g k e -> p (g k) e"),
                      Dm["w_pool"].rearrange("g (k p) e -> p (g k) e", p=128), [], [wpf.b])
            self.load(scb.t[:].rearrange("p g e -> p (g e)"),
                      Dm["pool_scale"].rearrange("g e -> (g e)").partition_broadcast(128), [], [scb.b])
            for g_ in range(4):
                self.tt("dve", wp.t[:, g_, :, :], wpf.t[:, g_, :, :],
                        scb.t[:, g_, :].unsqueeze(1).to_broadcast([128, 2, 256]), ALU.mult, [wpf.b, scb.b], [wp.b])
            mixT = self.sb(es, "mixT", [128, 8, 128], BF16)
            postage = self.rot(es, "postage", [128, 8, 256], BF16, 2)
            tiles = [("halo", 15)] + [("own", i) for i in range(16)]
        else:
            vtok = self.rot(es, "vtok", [128, 1024], BF16, 2)
            tiles = [("halo", i) for i in range(16)] + [("own", i) for i in range(16)]

        p_prev = None
        cur_stage = None
        cur_po = None
        deferred = []
        nt_ = len(tiles)
        xt_of, xnT_of = {}, {}

        def issue_load(j):
            kind, ti = tiles[j]
            xsrc = Dm["x_halo"] if kind == "halo" else Dm["x_own"]
            xt = xts.next()
            xt_of[j] = xt
            self.load(xt.t[:], xsrc[ti * 128:(ti + 1) * 128, :], [], [xt.b])

        issue_load(0)
        if nt_ > 1:
            issue_load(1)
        xn_next = self.norm_A(xt_of[0].t[:], xt_of[0].b, ns)
        xnT_of[0] = xnTs.next()
        self.norm_B(xn_next, lambda a, b_, x_=xnT_of[0]: x_.t[:, a:b_, :], xnT_of[0].b, idb, psb, 0, 1)
        for j, (kind, ti) in enumerate(tiles):
            gt = ti if kind == "halo" else 16 + ti
            if j + 2 < nt_:
                issue_load(j + 2)
            for f in deferred:
                f()
            deferred = []
            xnT = xnT_of.pop(j)
            if j + 1 < nt_:
                xn_next = self.norm_A(xt_of[j + 1].t[:], xt_of[j + 1].b, ns)
            if mode == "pq" and kind == "halo":
                cbs = [0, 1]
            else:
                cbs = [0, 1, 2, 3]
            for cb in cbs:
                for cc in range(16):
                    self.mm(pf[:, cb * 512:(cb + 1) * 512], xnT.t[:, cc, :], W.t[:, cc, cb * 512:(cb + 1) * 512],
                            cc == 0, cc == 15, [xnT.b, W.b], [pfb[cb]])
            if j + 1 < nt_:
                xnT_of[j + 1] = xnTs.next()
                self.norm_B(xn_next, lambda a, b_, x_=xnT_of[j + 1]: x_.t[:, a:b_, :], xnT_of[j + 1].b, idb, psb, 0, 1)
                xt_of.pop(j, None)
            if mode == "pq":
                p_cur = ptok.next()
                self.dve_copy(p_cur.t[:], pf[:, 0:1024], [pfb[0], pfb[1]], [p_cur.b])
                if kind == "own":
                    zs = zss.next()
                    self.dve_copy(zs.t[:], pf[:, 1024:2048], [pfb[2], pfb[3]], [zs.b])
                    qtok = rtok.next()
                    self.rope_apply(zs, qtok, cs, gt, tmp)
                    if ti % 2 == 0:
                        cur_stage = tstage.next()
                        cur_po = postage.next()
                    for h in range(8):
                        self.tr(pbt[:, h * 128:(h + 1) * 128], qtok.t[:, h * 128:(h + 1) * 128], idb.t[:],
                                [qtok.b, idb.b], [pbb[0]])
                    self.act_copy(cur_stage.t[:, :, (ti % 2) * 128:(ti % 2 + 1) * 128],
                                  pbt[:, 0:1024].rearrange("p (h t) -> p h t", h=8), [pbb[0]], [cur_stage.b])
                    var = 0 if ti == 0 else 1
                    for cc in range(8):
                        g = cc // 2
                        bank = 4 + cc // 4
                        o = pf[:, 2048 + cc * 128: 2048 + (cc + 1) * 128]
                        self.mm(o, p_prev.t[:, cc * 128:(cc + 1) * 128], A.t[:, g, 2, :], cc % 4 == 0, False,
                                [p_prev.b, A.b], [pfb[bank]])
                        self.mm(o, p_cur.t[:, cc * 128:(cc + 1) * 128], A.t[:, g, var, :], False, True,
                                [p_cur.b, A.b], [pfb[bank]])
                    self.dve_copy(mixT.t[:], pf[:, 2048:3072].rearrange("p (c t) -> p c t", c=8),
                                  [pfb[4], pfb[5]], [mixT.b])
                    for ec in range(8):
                        g = ec // 2
                        bank = 4 + ec // 4
                        o = pf[:, 2048 + ec * 128: 2048 + (ec + 1) * 128]
                        for kc in range(2):
                            self.mm(o, wp.t[:, g, kc, (ec % 2) * 128:(ec % 2 + 1) * 128], mixT.t[:, 2 * g + kc, :],
                                    (ec % 4 == 0 and kc == 0), kc == 1, [wp.b, mixT.b], [pfb[bank]])
                    self.act_copy(cur_po.t[:, :, (ti % 2) * 128:(ti % 2 + 1) * 128],
                                  pf[:, 2048:3072].rearrange("p (c t) -> p c t", c=8), [pfb[4], pfb[5]], [cur_po.b])
                    if ti % 2 == 1:
                        blk = ti // 2

                        def st_q(blk=blk, stg_=cur_stage, po_=cur_po):
                            self.load(Dm["qT_d"][:, :, blk * 256:(blk + 1) * 256].rearrange("h p t -> p h t"),
                                      stg_.t[:], [stg_.b], [DB["qT_d"]], cname="st")
                            self.load(Dm["mixT_d"][0:8, :, blk * 256:(blk + 1) * 256].rearrange("c p t -> p c t"),
                                      po_.t[:], [po_.b], [DB["mixT_d"]], cname="st")
                        deferred.append(st_q)
                p_prev = p_cur
            else:
                zs = zss.next()
                self.dve_copy(zs.t[:], pf[:, 0:1024], [pfb[0], pfb[1]], [zs.b])
                ktok = rtok.next()
                self.rope_apply(zs, ktok, cs, gt, tmp)
                if gt % 2 == 0:
                    cur_stage = tstage.next()
                for h in range(8):
                    self.tr(pbt[:, h * 128:(h + 1) * 128], ktok.t[:, h * 128:(h + 1) * 128], idb.t[:],
                            [ktok.b, idb.b], [pbb[0]])
                self.act_copy(cur_stage.t[:, :, (gt % 2) * 128:(gt % 2 + 1) * 128],
                              pbt[:, 0:1024].rearrange("p (h t) -> p h t", h=8), [pbb[0]], [cur_stage.b])
                v = vtok.next()
                self.dve_copy(v.t[:], pf[:, 1024:2048], [pfb[2], pfb[3]], [v.b])

                def st_v(gt=gt, v=v):
                    self.load(Dm["v_d"][gt * 128:(gt + 1) * 128, :], v.t[:], [v.b], [DB["v_d"]], cname="st")
                deferred.append(st_v)
                if gt % 2 == 1:
                    blk = gt // 2

                    def st_k(blk=blk, stg_=cur_stage):
                        self.load(Dm["kT_d"][:, :, blk * 256:(blk + 1) * 256].rearrange("h p t -> p h t"),
                                  stg_.t[:], [stg_.b], [DB["kT_d"]], cname="st")
                    deferred.append(st_k)
        for f in deferred:
            f()

    def st_attn(self, es):
        Dm, DB, P = self.D, self.DB, self.P
        (pf, pfb), _ = self.psum(es, 8, 0)
        masks = self.sb(es, "masks", [128, 3, 512], BF16)
        ones = self.sb(es, "ones", [128, 128], BF16)
        self.load(masks.t[:], Dm["masks"][:, :, :], [], [masks.b])
        self.load(ones.t[:], Dm["ones_bf"][:, :], [], [ones.b])
        qTs = self.rot(es, "qTh", [128, NT], BF16, 2)
        kTs = self.rot(es, "kTh", [128, NT + NH], BF16, 2)
        Vs = self.rot(es, "Vh", [128, 69, 128], BF16, 2)
        Oacc = self.sb(es, "Oacc", [128, NT], F32)
        Lacc = self.sb(es, "Lacc", [128, NT], F32)
        Pts = self.rot(es, "Pt", [128, 2, 512], BF16, 3)
        outs = self.rot(es, "attn_o", [128, NT], BF16, 2)
        scale = 128.0 ** -0.5

        def head_loads(h):
            qT, kT, V = qTs.next(), kTs.next(), Vs.next()
            self.load(qT.t[:], Dm["qT_d"][h], [DB["qT_d"]], [qT.b])
            self.load(kT.t[:], Dm["kT_d"][h], [DB["kT_d"]], [kT.b])
            vh = Dm["v_d"][:, h * 128:(h + 1) * 128]
            self.load(V.t[:, 0:17, :], vh[1920:4096, :].rearrange("(j p) c -> p j c", p=128), [DB["v_d"]], [V.b])
            v4 = vh.rearrange("(a d) c -> d a c", d=4)
            for r in range(4):
                self.load(V.t[:, 17 + r * 5: 17 + (r + 1) * 5, :],
                          v4[r, 384:1024, :].rearrange("(m p) c -> p m c", p=128), [DB["v_d"]], [V.b])
            v16 = vh.rearrange("(a d) c -> d a c", d=16)
            for r in range(16):
                self.load(V.t[:, 37 + r * 2: 37 + (r + 1) * 2, :],
                          v16[r, :, :].rearrange("(m p) c -> p m c", p=128), [DB["v_d"]], [V.b])
            return qT, kT, V

        def head_groups(qT, kT, V):
            k1 = kT.t[:]
            k4 = kT.t[:].rearrange("p (a d) -> p d a", d=4)
            k16 = kT.t[:].rearrange("p (a d) -> p d a", d=16)
            q1 = qT.t[:]
            q4 = qT.t[:].rearrange("p (a d) -> p d a", d=4)
            q16 = qT.t[:].rearrange("p (a d) -> p d a", d=16)
            groups = []
            for gi in range(4):
                units = []
                for u in range(4):
                    n = 4 * gi + u
                    units.append((k1[:, 1920 + 128 * n: 2048 + 128 * n], k1[:, 2048 + 128 * n: 2176 + 128 * n],
                                  q1[:, 128 * n:128 * (n + 1)], n, n + 1))
                mv = (2, 0) if gi == 0 else (0, 0)
                groups.append((units, mv, lambda acc, gi=gi: acc[:, 512 * gi:512 * (gi + 1)].rearrange("p (u q) -> p u q", u=4)))
            for n in range(4):
                units = []
                for r in range(4):
                    units.append((k4[:, r, 384 + 128 * n: 512 + 128 * n], k4[:, r, 512 + 128 * n: 640 + 128 * n],
                                  q4[:, r, 128 * n:128 * (n + 1)], 17 + r * 5 + n, 17 + r * 5 + n + 1))
                mv = (1, 1) if n == 0 else (0, 0)
                groups.append((units, mv, lambda acc, n=n: acc[:, 512 * n:512 * (n + 1)].rearrange("p (q r) -> p r q", r=4)))
            for gi in range(4):
                units = []
                for u in range(4):
                    r = 4 * gi + u
                    units.append((k16[:, r, 0:128], k16[:, r, 128:256], q16[:, r, 0:128], 37 + r * 2, 38 + r * 2))
                groups.append((units, (1, 1), lambda acc, gi=gi: acc[:].rearrange("p (q r) -> p r q", r=16)[:, 4 * gi:4 * gi + 4, :]))
            return groups

        items = []
        res = {0: head_loads(0)}

        def s_part(idx):
            h, gi = idx // 12, idx % 12
            if gi == 1 and h + 1 < 8:
                res[h + 1] = head_loads(h + 1)
            qT, kT, V = res[h]
            units, mv, accv = head_groups(qT, kT, V)[gi]
            s_ = idx % 2
            bSA, bSB = 4 * s_, 4 * s_ + 1
            for u, (kp, kc, qq, vp, vc) in enumerate(units):
                bank = bSA if u < 2 else bSB
                base = bank * 512 + (u % 2) * 256
                self.mm(pf[:, base:base + 128], kp, qq, True, True, [kT.b, qT.b], [pfb[bank]])
                self.mm(pf[:, base + 128:base + 256], kc, qq, True, True, [kT.b, qT.b], [pfb[bank]])
            Pt = Pts.next()
            for j, bank in enumerate((bSA, bSB)):
                P.op("act", lambda e, j=j, bank=bank, Pt=Pt: e.activation(
                    out=Pt.t[:, j, :], in_=pf[:, bank * 512:(bank + 1) * 512], func=AF.Exp, scale=scale),
                    reads=[pfb[bank]], writes=[Pt.b])
            for j in range(2):
                self.tt("dve", Pt.t[:, j, :], Pt.t[:, j, :], masks.t[:, mv[j], :], ALU.mult,
                        [Pt.b, masks.b], [Pt.b])
            return (units, accv, V, Pt, s_)

        def o_part(idx, st_):
            h, gi = idx // 12, idx % 12
            units, accv, V, Pt, s_ = st_
            bO, bL = 4 * s_ + 2, 4 * s_ + 3
            if gi == 0:
                P.op("dve", lambda e: e.memset(Oacc.t[:], 0.0), writes=[Oacc.b])
                P.op("dve", lambda e: e.memset(Lacc.t[:], 0.0), writes=[Lacc.b])
            for u, (kp, kc, qq, vp, vc) in enumerate(units):
                j, o = u // 2, (u % 2) * 256
                pp = Pt.t[:, j, o:o + 128]
                pc = Pt.t[:, j, o + 128:o + 256]
                oo = pf[:, bO * 512 + u * 128: bO * 512 + (u + 1) * 128]
                ll = pf[:, bL * 512 + u * 128: bL * 512 + (u + 1) * 128]
                self.mm(oo, V.t[:, vp, :], pp, u == 0, False, [V.b, Pt.b], [pfb[bO]])
                self.mm(oo, V.t[:, vc, :], pc, False, True, [V.b, Pt.b], [pfb[bO]])
                self.mm(ll, ones.t[:], pp, u == 0, False, [ones.b, Pt.b], [pfb[bL]])
                self.mm(ll, ones.t[:], pc, False, True, [ones.b, Pt.b], [pfb[bL]])
            ov = accv(Oacc.t)
            lv = accv(Lacc.t)
            self.tt("dve", ov, ov, pf[:, bO * 512:(bO + 1) * 512].rearrange("p (u q) -> p u q", u=4), ALU.add,
                    [Oacc.b, pfb[bO]], [Oacc.b])
            self.tt("dve", lv, lv, pf[:, bL * 512:(bL + 1) * 512].rearrange("p (u q) -> p u q", u=4), ALU.add,
                    [Lacc.b, pfb[bL]], [Lacc.b])
            if gi == 11:
                o = outs.next()
                P.op("dve", lambda e: e.reciprocal(out=Lacc.t[:], in_=Lacc.t[:]), reads=[Lacc.b], writes=[Lacc.b])
                self.tt("dve", o.t[:], Oacc.t[:], Lacc.t[:], ALU.mult, [Oacc.b, Lacc.b], [o.b])
                self.load(Dm["mixT_d"][8 + h], o.t[:], [o.b], [DB["mixT_d"]], cname="st")
                res.pop(h, None)

        nit = 96
        pend = s_part(0)
        for idx in range(nit):
            nxt = s_part(idx + 1) if idx + 1 < nit else None
            o_part(idx, pend)
            pend = nxt

    def st_mem(self, es):
        Dm, DB, P = self.D, self.DB, self.P
        (pf, pfb), (pbt, pbb) = self.psum(es, 4, 2)
        c = self.consts(es)
        idb = c["idb"]
        ns = self.norm_scratch(es)
        gm = self.sb(es, "gmem", [128, 16], F32)
        self.load(gm.t[:], Dm["g_mem_t"][:, :], [], [gm.b])
        wck = self.sb(es, "wck", [128, 16, 512], BF16)
        wcv = self.sb(es, "wcv", [128, 16, 512], BF16)
        stg = self.rot(es, "wstg", [128, 512], F32, 2)
        self.load_w(wck, lambda cc: Dm["w_ck"][cc * 128:(cc + 1) * 128, :], 16, 512, gm, stg)
        self.load_w(wcv, lambda cc: Dm["w_cv"][cc * 128:(cc + 1) * 128, :], 16, 512, gm, stg)
        memT = self.sb(es, "memT", [128, 16, 256], BF16)
        xts = self.rot(es, "xt", [128, D], F32, 2)
        for t in range(2):
            xt = xts.next()
            self.load(xt.t[:], Dm["mem"][t * 128:(t + 1) * 128, :], [], [xt.b])
            self.norm_T(xt.t[:], xt.b, lambda a, b_, t=t: memT.t[:, a:b_, t * 128:(t + 1) * 128], memT.b, ns, idb,
                        (pbt, pbb), 0, 1)
        kcT = self.sb(es, "kcT", [128, 4, 256], BF16)
        vc = self.sb(es, "vc", [128, 2, 512], BF16)
        for h in range(4):
            bank = h // 2
            o = pf[:, h * 256:(h + 1) * 256]
            for cc in range(16):
                self.mm(o, wck.t[:, cc, h * 128:(h + 1) * 128], memT.t[:, cc, :], (cc == 0 and h % 2 == 0), cc == 15,
                        [wck.b, memT.b], [pfb[bank]])
        self.act_copy(kcT.t[:], pf[:, 0:1024].rearrange("p (h m) -> p h m", h=4), [pfb[0], pfb[1]], [kcT.b])
        for t in range(2):
            o = pf[:, 1024 + t * 512: 1024 + (t + 1) * 512]
            for cc in range(16):
                self.mm(o, memT.t[:, cc, t * 128:(t + 1) * 128], wcv.t[:, cc, :], cc == 0, cc == 15,
                        [wcv.b, memT.b], [pfb[2 + t]])
        self.dve_copy(vc.t[:], pf[:, 1024:2048].rearrange("p (t e) -> p t e", t=2), [pfb[2], pfb[3]], [vc.b])
        self.load(Dm["kcT_d"][:, :, :], kcT.t[:], [kcT.b], [DB["kcT_d"]], cname="st")
        self.load(Dm["vc_d"][:, :, :], vc.t[:], [vc.b], [DB["vc_d"]], cname="st")

    def st_out_cross(self, es):
        Dm, DB, P = self.D, self.DB, self.P
        (pf, pfb), (pbt, pbb) = self.psum(es, 6, 2)
        c = self.consts(es)
        idb = c["idb"]
        ns = self.norm_scratch(es)
        ones = self.sb(es, "ones", [128, 128], BF16)
        self.load(ones.t[:], Dm["ones_bf"][:, :], [], [ones.b])
        gc = self.sb(es, "gcross", [128, 16], F32)
        self.load(gc.t[:], Dm["g_cross_t"][:, :], [], [gc.b])
        stg = self.rot(es, "wstg", [128, 2048], F32, 2)
        Wo = self.sb(es, "Wo", [128, 16, 2048], BF16)
        self.load_w(Wo, lambda cc: Dm["w_out"][cc * 128:(cc + 1) * 128, :], 16, 2048, None, stg)
        wcq = self.sb(es, "wcq", [128, 16, 512], BF16)
        self.load_w(wcq, lambda cc: Dm["w_cq"][cc * 128:(cc + 1) * 128, :], 16, 512, gc, stg)
        wco = self.sb(es, "wco", [128, 4, 2048], BF16)
        self.load_w(wco, lambda cc: Dm["w_co"][cc * 128:(cc + 1) * 128, :], 4, 2048, None, stg)
        kcT = self.sb(es, "kcT", [128, 4, 256], BF16)
        vc = self.sb(es, "vc", [128, 2, 512], BF16)
        self.load(kcT.t[:], Dm["kcT_d"][:, :, :], [DB["kcT_d"]], [kcT.b])
        self.load(vc.t[:], Dm["vc_d"][:, :, :], [DB["vc_d"]], [vc.b])
        TB = 256
        NTL = TB // 128
        mixTs = self.rot(es, "mixTb", [128, 16, TB], BF16, 2)
        x1s = self.rot(es, "x1b", [128, NTL, D], F32, 1)
        h2T = self.sb(es, "h2T", [128, 16, TB], BF16)
        qcT = self.sb(es, "qcT", [128, 4, TB], BF16)
        ocT = self.sb(es, "ocT", [128, 4, TB], BF16)
        PT = self.rot(es, "PTc", [128, 2, TB], BF16, 2)
        rL = self.sb(es, "rL", [128, TB], F32)
        x2s = self.rot(es, "x2t", [128, D], F32, 2)
        scale = 128.0 ** -0.5
        cstb = self.rot(es, "cstb", [128, 2048], BF16, 2)
        for blk in range(NT // TB):
            mT = mixTs.next()
            self.load(mT.t[:], Dm["mixT_d"][:, :, blk * TB:(blk + 1) * TB].rearrange("c p t -> p c t"),
                      [DB["mixT_d"]], [mT.b])
            x1 = x1s.next()
            self.load(x1.t[:], Dm["x_own"][blk * TB:(blk + 1) * TB, :].rearrange("(t p) d -> p t d", p=128),
                      [], [x1.b])
            self.cast_chunks(8, stg, cstb)
            for t in range(NTL):
                for cb in range(4):
                    for cc in range(16):
                        self.mm(pf[:, cb * 512:(cb + 1) * 512], mT.t[:, cc, t * 128:(t + 1) * 128],
                                Wo.t[:, cc, cb * 512:(cb + 1) * 512], cc == 0, cc == 15, [mT.b, Wo.b], [pfb[cb]])
                self.tt("dve", x1.t[:, t, :], x1.t[:, t, :], pf[:, 0:2048], ALU.add,
                        [x1.b, pfb[0], pfb[1], pfb[2], pfb[3]], [x1.b])
                xn_t = self.norm_A(x1.t[:, t, :], x1.b, ns)
                if t > 0:
                    self.norm_B(xn_prev, lambda a, b_, t=t - 1: h2T.t[:, a:b_, t * 128:(t + 1) * 128], h2T.b,
                                idb, (pbt, pbb), 0, 1)
                xn_prev = xn_t
            self.norm_B(xn_prev, lambda a, b_, t=NTL - 1: h2T.t[:, a:b_, t * 128:(t + 1) * 128], h2T.b,
                        idb, (pbt, pbb), 0, 1)
            for h in range(4):
                bank = (h * TB) // 512
                for cc in range(16):
                    self.mm(pf[:, h * TB:(h + 1) * TB], wcq.t[:, cc, h * 128:(h + 1) * 128], h2T.t[:, cc, :],
                            (cc == 0 and (h * TB) % 512 == 0), cc == 15, [wcq.b, h2T.b], [pfb[bank]])
            self.act_copy(qcT.t[:], pf[:, 0:4 * TB].rearrange("p (h t) -> p h t", h=4), [pfb[0], pfb[1]], [qcT.b])
            def c_s(h):
                pt = PT.next()
                bS = 4 + (h % 2)
                for mc in range(2):
                    self.mm(pf[:, bS * 512 + mc * TB: bS * 512 + (mc + 1) * TB], kcT.t[:, h, mc * 128:(mc + 1) * 128],
                            qcT.t[:, h, :], mc == 0, True, [kcT.b, qcT.b], [pfb[bS]])
                P.op("act", lambda e, bS=bS, pt=pt: e.activation(
                    out=pt.t[:].rearrange("p m t -> p (m t)"), in_=pf[:, bS * 512: bS * 512 + 2 * TB], func=AF.Exp,
                    scale=scale), reads=[pfb[bS]], writes=[pt.b])
                return pt

            def c_o(h, pt):
                bO = 2 + (h % 2)
                oo = pf[:, bO * 512: bO * 512 + TB]
                ll = pf[:, bO * 512 + TB: bO * 512 + 2 * TB]
                for mc in range(2):
                    self.mm(oo, vc.t[:, mc, h * 128:(h + 1) * 128], pt.t[:, mc, :], mc == 0, mc == 1,
                            [vc.b, pt.b], [pfb[bO]])
                for mc in range(2):
                    self.mm(ll, ones.t[:], pt.t[:, mc, :], False, mc == 1, [ones.b, pt.b], [pfb[bO]])
                P.op("dve", lambda e, ll=ll: e.reciprocal(out=rL.t[:], in_=ll), reads=[pfb[bO]], writes=[rL.b])
                self.tt("dve", ocT.t[:, h, :], rL.t[:], oo, ALU.mult, [rL.b, pfb[bO]], [ocT.b])

            ptn = c_s(0)
            for h in range(4):
                ptc = ptn
                if h + 1 < 4:
                    ptn = c_s(h + 1)
                c_o(h, ptc)
            for t in range(NTL):
                for cb in range(4):
                    for hh in range(4):
                        self.mm(pf[:, cb * 512:(cb + 1) * 512], ocT.t[:, hh, t * 128:(t + 1) * 128],
                                wco.t[:, hh, cb * 512:(cb + 1) * 512], hh == 0, hh == 3, [ocT.b, wco.b], [pfb[cb]])
                x2 = x2s.next()
                self.tt("dve", x2.t[:], x1.t[:, t, :], pf[:, 0:2048], ALU.add,
                        [x1.b, pfb[0], pfb[1], pfb[2], pfb[3]], [x2.b])
                row = blk * TB + t * 128
                self.load(Dm["x2_d"][row:row + 128, :], x2.t[:], [x2.b], [DB["x2_d"]], cname="st")

    def st_route(self, es):
        Dm, DB, P = self.D, self.DB, self.P
        (pf, pfb), (pbt, pbb) = self.psum(es, 6, 2)
        c = self.consts(es)
        idb = c["idb"]
        idf = self.sb(es, "idf", [128, 128], F32)
        self.load(idf.t[:], Dm["ident_f"][:, :], [], [idf.b])
        io16 = self.sb(es, "io16", [128, 16], F32)
        self.load(io16.t[:], Dm["iota16"][:, :], [], [io16.b])
        ns = self.norm_scratch(es)
        gf = self.sb(es, "gffn", [128, 16], F32)
        self.load(gf.t[:], Dm["g_ffn_t"][:, :], [], [gf.b])
        stg = self.rot(es, "wstg", [128, 2048], F32, 2)
        Wq = self.sb(es, "Wq", [128, 16, 2048], BF16)
        self.load_w(Wq, lambda cc: Dm["w_pq"][cc * 128:(cc + 1) * 128, :], 16, 2048, gf, stg)
        sk = self.sb(es, "sk", [128, 2, 128], BF16)
        skf = self.sb(es, "skf", [128, 2, 128], F32)
        self.load(skf.t[:, 0, :], Dm["skT1"][:, :], [], [skf.b])
        self.load(skf.t[:, 1, :], Dm["skT2"][:, :], [], [skf.b])
        self.dve_copy(sk.t[:], skf.t[:], [skf.b], [sk.b])
        xts = self.rot(es, "xt", [128, D], F32, 2)
        h3Ts = self.rot(es, "h3T", [128, 16, 512], BF16, 1)
        qTs = self.sb(es, "qTs", [128, 16, 512], BF16)
        tmpk = self.sb(es, "tmpk", [128, 16, 128], F32)
        v12 = self.sb(es, "v12", [128, 2, 8, 16], F32)
        i12 = self.sb(es, "i12", [128, 2, 8, 16], U32)
        i12f = self.sb(es, "i12f", [128, 2, 8, 16], F32)
        cand = self.sb(es, "cand", [128, 8, 16, 16], F32)
        eqs = Rot([self.sb(es, "eq", [128, 8, 16, 16], F32), cand])
        sc = self.sb(es, "sc", [128, 8, 16], F32)
        ci = self.sb(es, "ci", [128, 8, 16], U32)
        rr = self.sb(es, "rr", [128, 2, 8, 16], U32)
        rrf = self.sb(es, "rrf", [128, 2, 8, 16], F32)
        ex = self.sb(es, "ex", [128, 8, 16], F32)
        zz = self.sb(es, "zz", [128, 8], F32)
        egs = self.rot(es, "eg", [128, 3, 128], F32, 2)
        routs = self.rot(es, "rout", [128, 3, 128], F32, 2)
        NEG = -1.0e30
        s12s = self.rot(es, "s12", [128, 2, 8, 128], F32, 2)
        state = {}

        def block_prep(blk):
            h3T = h3Ts.next()
            for t in range(4):
                xt = xts.next()
                row = blk * 512 + t * 128
                self.load(xt.t[:], Dm["x2_d"][row:row + 128, :], [DB["x2_d"]], [xt.b])
                self.norm_T(xt.t[:], xt.b, lambda a, b_, t=t, h3T=h3T: h3T.t[:, a:b_, t * 128:(t + 1) * 128], h3T.b,
                            ns, idb, (pbt, pbb), 0, 1)
            self.load(Dm["h3T_d"][:, :, blk * 512:(blk + 1) * 512].rearrange("c p t -> p c t"), h3T.t[:],
                      [h3T.b], [DB["h3T_d"]], cname="st")
            for ch in range(16):
                bank = ch % 4
                for cc in range(16):
                    self.mm(pf[:, bank * 512:(bank + 1) * 512], Wq.t[:, cc, ch * 128:(ch + 1) * 128], h3T.t[:, cc, :],
                            cc == 0, cc == 15, [Wq.b, h3T.b], [pfb[bank]])
                self.act_copy(qTs.t[:, ch, :], pf[:, bank * 512:(bank + 1) * 512], [pfb[bank]], [qTs.b])

        def front(i):
            if i % 4 == 0:
                block_prep(i // 4)
            t = i % 4
            s12 = s12s.next()
            state[i] = s12
            for half in range(2):
                for h in range(8):
                    bank = half * 2 + h // 4
                    o = pf[:, bank * 512 + (h % 4) * 128: bank * 512 + (h % 4 + 1) * 128]
                    self.mm(o, qTs.t[:, 2 * h + half, t * 128:(t + 1) * 128], sk.t[:, half, :], h % 4 == 0, True,
                            [qTs.b, sk.b], [pfb[bank]])
            self.act_copy(s12.t[:].rearrange("p a h k -> p (a h k)"), pf[:, 0:2048],
                          [pfb[0], pfb[1], pfb[2], pfb[3]], [s12.b])

        def back(i):
            tile_i = i
            s12 = state.pop(i)
            chains = [(half, h) for h in range(8) for half in range(2)]
            for (half, h) in chains:
                sv = s12.t[:, half, h, :]
                P.op("dve", lambda e, sv=sv, half=half, h=h: e.max(out=v12.t[:, half, h, 0:8], in_=sv),
                     reads=[s12.b], writes=[v12.b])
            for ci_, (half, h) in enumerate(chains):
                sv = s12.t[:, half, h, :]
                P.op("dve", lambda e, sv=sv, half=half, h=h, ci_=ci_: e.match_replace(
                    out=tmpk.t[:, ci_, 0:128], in_to_replace=v12.t[:, half, h, 0:8], in_values=sv, imm_value=NEG),
                    reads=[s12.b, v12.b], writes=[tmpk.b])
            for ci_, (half, h) in enumerate(chains):
                P.op("dve", lambda e, half=half, h=h, ci_=ci_: e.max(out=v12.t[:, half, h, 8:16], in_=tmpk.t[:, ci_, 0:128]),
                     reads=[tmpk.b], writes=[v12.b])
            for k0 in (0, 8):
                for (half, h) in chains:
                    sv = s12.t[:, half, h, :]
                    P.op("dve", lambda e, sv=sv, half=half, h=h, k0=k0: e.max_index(
                        out=i12.t[:, half, h, k0:k0 + 8], in_max=v12.t[:, half, h, k0:k0 + 8], in_values=sv),
                        reads=[s12.b, v12.b], writes=[i12.b])
            self.tt("dve", cand.t[:], v12.t[:, 0, :, :].unsqueeze(3).to_broadcast([128, 8, 16, 16]),
                    v12.t[:, 1, :, :].unsqueeze(2).to_broadcast([128, 8, 16, 16]), ALU.add, [v12.b], [cand.b])
            cvs = [cand.t[:, h, :, :].rearrange("p a b -> p (a b)") for h in range(8)]
            for h in range(8):
                P.op("dve", lambda e, cv=cvs[h], h=h: e.max(out=sc.t[:, h, 0:8], in_=cv), reads=[cand.b], writes=[sc.b])
            for h in range(8):
                P.op("dve", lambda e, cv=cvs[h], h=h: e.match_replace(
                    out=tmpk.t[:, 2 * h:2 * h + 2, :].rearrange("p a b -> p (a b)"), in_to_replace=sc.t[:, h, 0:8],
                    in_values=cv, imm_value=NEG), reads=[cand.b, sc.b], writes=[tmpk.b])
            for h in range(8):
                P.op("dve", lambda e, h=h: e.max(out=sc.t[:, h, 8:16],
                                                 in_=tmpk.t[:, 2 * h:2 * h + 2, :].rearrange("p a b -> p (a b)")),
                     reads=[tmpk.b], writes=[sc.b])
            for k0 in (0, 8):
                for h in range(8):
                    P.op("dve", lambda e, cv=cvs[h], h=h, k0=k0: e.max_index(
                        out=ci.t[:, h, k0:k0 + 8], in_max=sc.t[:, h, k0:k0 + 8], in_values=cv),
                        reads=[cand.b, sc.b], writes=[ci.b])
            eg = egs.next()
            self.tt("dve", ex.t[:], sc.t[:], sc.t[:, :, 0:1].to_broadcast([128, 8, 16]), ALU.subtract,
                    [sc.b], [ex.b])
            P.op("act", lambda e: e.activation(out=ex.t[:], in_=ex.t[:], func=AF.Exp), reads=[ex.b], writes=[ex.b])
            P.op("dve", lambda e: e.tensor_single_scalar(out=rr.t[:, 0, :, :], in_=ci.t[:], scalar=4,
                                                        op=ALU.logical_shift_right), reads=[ci.b], writes=[rr.b])
            P.op("dve", lambda e: e.tensor_single_scalar(out=rr.t[:, 1, :, :], in_=ci.t[:], scalar=15,
                                                        op=ALU.bitwise_and), reads=[ci.b], writes=[rr.b])
            self.dve_copy(rrf.t[:], rr.t[:], [rr.b], [rrf.b])
            self.dve_copy(i12f.t[:], i12.t[:], [i12.b], [i12f.b])
            for half in range(2):
                eq_ = eqs.next()
                self.tt("dve", eq_.t[:], rrf.t[:, half, :, :].unsqueeze(3).to_broadcast([128, 8, 16, 16]),
                        io16.t[:].unsqueeze(1).unsqueeze(1).to_broadcast([128, 8, 16, 16]), ALU.is_equal,
                        [rrf.b, io16.b], [eq_.b])
            for half in range(2):
                eq_ = eqs.next()
                self.tt("dve", eq_.t[:], eq_.t[:], i12f.t[:, half, :, :].unsqueeze(2).to_broadcast([128, 8, 16, 16]),
                        ALU.mult, [eq_.b, i12f.b], [eq_.b])
            for half in range(2):
                eq_ = eqs.next()
                P.op("dve", lambda e, half=half, eg=eg, eq_=eq_: e.reduce_sum(
                    out=eg.t[:, half, :].rearrange("p (h k) -> p h k", h=8), in_=eq_.t[:], axis=AX.X),
                    reads=[eq_.b], writes=[eg.b])
            P.op("dve", lambda e: e.reduce_sum(out=zz.t[:], in_=ex.t[:], axis=AX.X), reads=[ex.b], writes=[zz.b])
            P.op("dve", lambda e: e.reciprocal(out=zz.t[:], in_=zz.t[:]), reads=[zz.b], writes=[zz.b])
            self.tt("dve", eg.t[:, 2, :].rearrange("p (h k) -> p h k", h=8), ex.t[:],
                    zz.t[:].unsqueeze(2).to_broadcast([128, 8, 16]), ALU.mult, [ex.b, zz.b], [eg.b])
            for j in range(3):
                self.tr(pf[:, 4 * 512 + j * 128: 4 * 512 + (j + 1) * 128], eg.t[:, j, :], idf.t[:],
                        [eg.b, idf.b], [pfb[4]])
            ro = routs.next()
            self.act_copy(ro.t[:].rearrange("p j t -> p (j t)"), pf[:, 2048:2048 + 384], [pfb[4]], [ro.b])
            self.load(Dm["route_d"][tile_i], ro.t[:], [ro.b], [DB["route_d"]], cname="st")

        cstb = self.rot(es, "cstb", [128, 2048], BF16, 2)
        front(0)
        for i in range(16):
            if i + 1 < 16:
                front(i + 1)
            self.cast_chunks(12, stg, cstb)
            back(i)
        self.cast_chunks(256, stg, cstb)

    def st_experts_v1(self, es):
        Dm, DB, P = self.D, self.DB, self.P
        (pf, pfb), _ = self.psum(es, 8, 0)
        io = self.sb(es, "io128", [128, 128], F32)
        self.load(io.t[:], Dm["iota128"][:, :], [], [io.b])
        iobf = self.sb(es, "iobf", [128, 128], BF16)
        self.dve_copy(iobf.t[:], io.t[:], [io.b], [iobf.b])
        gfin = self.sb(es, "gfin", [128, D], F32)
        self.load(gfin.t[:], Dm["g_final"][0, :].partition_broadcast(128), [], [gfin.b])
        GtH = self.rot(es, "GtH", [128, 64, 256], BF16, 2)
        h3Ts = self.rot(es, "h3Tb", [128, 16, 256], BF16, 2)
        routs = self.rot(es, "routb", [128, 2, 3, 128], F32, 2)
        robs = self.rot(es, "robf", [128, 2, 3, 128], BF16, 2)
        oh1 = self.rot(es, "oh1", [128, 8, 64], BF16, 2)
        oh2 = self.rot(es, "oh2", [128, 8, 128], BF16, 2)
        wus = self.rot(es, "wu", [128, 16, 512], BF16, 2)
        wvs = self.rot(es, "wv", [128, 4, 1024], BF16, 3)
        tA = self.rot(es, "tA", [128, 2, 256], BF16, 2)
        x3s = self.rot(es, "x3", [128, 2, D], F32, 2)
        ss = self.rot(es, "f_ss", [128, 1], F32, 2)
        outs = self.rot(es, "outt", [128, D], F32, 1)
        wuT = Dm["wuT_bf"].rearrange("(c p) e -> p c e", p=128)
        blkres = {}

        def block_loads(blk):
            h3T, ro, x3 = h3Ts.next(), routs.next(), x3s.next()
            tok0 = blk * 256
            self.load(h3T.t[:], Dm["h3T_d"][:, :, tok0:tok0 + 256].rearrange("c p t -> p c t"), [DB["h3T_d"]], [h3T.b])
            self.load(ro.t[:], Dm["route_d"][2 * blk:2 * blk + 2].rearrange("a s j t -> s a j t"),
                      [DB["route_d"]], [ro.b])
            self.load(x3.t[:], Dm["x2_d"][tok0:tok0 + 256, :].rearrange("(t p) d -> p t d", p=128), [DB["x2_d"]], [x3.b])
            blkres[blk] = (h3T, ro, x3)

        def g_batches(u):
            blk, hf = u // 2, u % 2
            if hf == 0:
                block_loads(blk)
            h3T, ro, x3 = blkres[blk]
            G = GtH.next()
            rb = robs.next()
            st_ = {"G": G}

            def prep():
                self.ts("dve", rb.t[:, :, 0, :], ro.t[:, :, 0, :], float(-64 * hf), None, ALU.add, None, [ro.b], [rb.b])
                self.dve_copy(rb.t[:, :, 1:3, :], ro.t[:, :, 1:3, :], [ro.b], [rb.b])
            outl = [prep]
            GB = 8
            for bi, t0 in enumerate(range(0, 256, GB)):
                def part1(bi=bi, t0=t0):
                    a_, tt_ = t0 // 128, t0 % 128
                    o1, o2 = oh1.next(), oh2.next()
                    io64 = iobf.t[:, 0:64].unsqueeze(1).to_broadcast([128, GB, 64])
                    io128 = iobf.t[:].unsqueeze(1).to_broadcast([128, GB, 128])
                    e1b = rb.t[:, a_, 0, tt_:tt_ + GB].unsqueeze(2).to_broadcast([128, GB, 64])
                    gb_ = rb.t[:, a_, 2, tt_:tt_ + GB].unsqueeze(2).to_broadcast([128, GB, 64])
                    e2b = rb.t[:, a_, 1, tt_:tt_ + GB].unsqueeze(2).to_broadcast([128, GB, 128])
                    self.tt("dve", o1.t[:], io64, e1b, ALU.is_equal, [iobf.b, rb.b], [o1.b])
                    self.tt("dve", o1.t[:], o1.t[:], gb_, ALU.mult, [o1.b, rb.b], [o1.b])
                    self.tt("dve", o2.t[:], io128, e2b, ALU.is_equal, [iobf.b, rb.b], [o2.b])

                    def part2(bi=bi, t0=t0, o1=o1, o2=o2):
                        bank = bi % 2
                        for j in range(GB):
                            self.mm(pf[:, bank * 512 + j * 64: bank * 512 + (j + 1) * 64], o2.t[:, j, :], o1.t[:, j, :],
                                    True, True, [o1.b, o2.b], [pfb[bank]])
                        self.act_copy(G.t[:, :, t0:t0 + GB].rearrange("p e t -> p t e"),
                                      pf[:, bank * 512:(bank + 1) * 512].rearrange("p (t e) -> p t e", t=GB),
                                      [pfb[bank]], [G.b])
                    return part2
                outl.append(part1)
            return st_, outl

        nun = 16
        pre_wu = {}
        deferred = []

        def issue_wu(u_, g_):
            hf_ = u_ % 2
            wu_ = wus.next()
            e0_ = (hf_ * 64 + 4 * g_) * 128
            self.load(wu_.t[:], wuT[:, :, e0_:e0_ + 512], [DB["wuT_bf"]], [wu_.b])
            pre_wu[(u_, g_)] = wu_

        cur, cur_list = g_batches(0)
        for f in cur_list:
            r_ = f()
            if r_ is not None:
                r_()
        for u in range(nun):
            blk, hf = u // 2, u % 2
            h3T, ro, x3 = blkres[blk]
            G = cur["G"]
            if u + 1 < nun:
                nxt, nxt_list = g_batches(u + 1)
            else:
                nxt, nxt_list = None, []
            for g in range(16):
                if (u, g) not in pre_wu:
                    issue_wu(u, g)
                wu = pre_wu.pop((u, g))
                for pr in range(2):
                    p_ = 2 * g + pr
                    bank = 2 + p_ % 2
                    pend = None
                    if nxt_list:
                        if p_ == 0:
                            nxt_list.pop(0)()
                        pend = nxt_list.pop(0)()
                    for j in range(2):
                        el = pr * 2 + j
                        for cc in range(16):
                            self.mm(pf[:, bank * 512 + j * 256: bank * 512 + (j + 1) * 256],
                                    wu.t[:, cc, el * 128:(el + 1) * 128], h3T.t[:, cc, :],
                                    (cc == 0 and j == 0), cc == 15, [wu.b, h3T.b], [pfb[bank]])
                    ta = tA.next()
                    P.op("act", lambda e, bank=bank, ta=ta: e.activation(
                        out=ta.t[:].rearrange("p j t -> p (j t)"), in_=pf[:, bank * 512:(bank + 1) * 512],
                        func=AF.Gelu_apprx_tanh), reads=[pfb[bank]], writes=[ta.b])
                    e1l = 4 * g + 2 * pr
                    self.tt("dve", G.t[:, e1l:e1l + 2, :], G.t[:, e1l:e1l + 2, :], ta.t[:], ALU.mult,
                            [G.b, ta.b], [G.b])
                    if pend is not None:
                        pend()
            assert not nxt_list
            if u + 1 < nun:
                issue_wu(u + 1, 0)
                issue_wu(u + 1, 1)
            for f in deferred:
                f()
            deferred = []
            for dh in range(2):
                for g in range(16):
                    wv = wvs.next()
                    r0 = (hf * 64 + 4 * g) * 128
                    self.load(wv.t[:], Dm["wv_bf"][r0:r0 + 512, dh * 1024:(dh + 1) * 1024]
                              .rearrange("(a p) d -> p a d", p=128), [DB["wv_bf"]], [wv.b])
                    for el in range(4):
                        e1l = 4 * g + el
                        for tl in range(2):
                            for db in range(2):
                                bank = 4 + tl * 2 + db
                                self.mm(pf[:, bank * 512:(bank + 1) * 512], G.t[:, e1l, tl * 128:(tl + 1) * 128],
                                        wv.t[:, el, db * 512:(db + 1) * 512], e1l == 0, e1l == 63,
                                        [G.b, wv.b], [pfb[bank]])
                for tl in range(2):
                    dst = x3.t[:, tl, dh * 1024:(dh + 1) * 1024]
                    self.tt("dve", dst, dst, pf[:, (4 + tl * 2) * 512:(6 + tl * 2) * 512], ALU.add,
                            [x3.b, pfb[4 + tl * 2], pfb[5 + tl * 2]], [x3.b])
            if hf == 1:
                tok0 = blk * 256
                for tl in range(2):
                    s_ = ss.next()
                    o = outs.next()
                    P.op("act", lambda e, tl=tl, s_=s_, o=o, x3=x3: e.activation(out=o.t[:], in_=x3.t[:, tl, :],
                                                                                 func=AF.Square, accum_out=s_.t[:]),
                         reads=[x3.b], writes=[o.b, s_.b])
                    P.op("act", lambda e, s_=s_: e.activation(out=s_.t[:], in_=s_.t[:], func=AF.Sqrt, scale=1.0 / D,
                                                              bias=self.eps.t[:]),
                         reads=[s_.b, self.eps.b], writes=[s_.b])
                    P.op("dve", lambda e, s_=s_: e.reciprocal(out=s_.t[:], in_=s_.t[:]), reads=[s_.b], writes=[s_.b])
                    P.op("dve", lambda e, tl=tl, s_=s_, o=o, x3=x3: e.scalar_tensor_tensor(
                        out=o.t[:], in0=x3.t[:, tl, :], scalar=s_.t[:, 0:1], in1=gfin.t[:], op0=ALU.mult, op1=ALU.mult),
                        reads=[x3.b, s_.b, gfin.b], writes=[o.b])
                    row = tok0 + tl * 128
                    self.load(Dm["out"][row:row + 128, :], o.t[:], [o.b], [DB["out"]], cname="st", queue="act")
                blkres.pop(blk)
            cur = nxt
        for f in deferred:
            f()


    def st_experts(self, es):
        Dm, DB, P = self.D, self.DB, self.P
        (pf, pfb), _ = self.psum(es, 8, 0)
        TB, NTL, NQ, EQ = 512, 4, 4, 32
        io = self.sb(es, "io128", [128, 128], F32)
        self.load(io.t[:], Dm["iota128"][:, :], [], [io.b])
        iobf = self.sb(es, "iobf", [128, 128], BF16)
        self.dve_copy(iobf.t[:], io.t[:], [io.b], [iobf.b])
        gfin = self.sb(es, "gfin", [128, D], F32)
        self.load(gfin.t[:], Dm["g_final"][0, :].partition_broadcast(128), [], [gfin.b])
        GtQ = self.rot(es, "GtQ", [128, EQ, TB], BF16, 2)
        h3Ts = self.rot(es, "h3Tb", [128, 16, TB], BF16, 1)
        routs = self.rot(es, "routb", [128, NTL, 3, 128], F32, 1)
        robs = self.rot(es, "robf", [128, NTL, 3, 128], BF16, 2)
        GB = 16
        oh1 = self.rot(es, "oh1", [128, GB, EQ], BF16, 2)
        oh2 = self.rot(es, "oh2", [128, GB, 128], BF16, 2)
        wus = self.rot(es, "wu", [128, 16, 512], BF16, 2)
        wvs = self.rot(es, "wv", [128, 4, 512], BF16, 3)
        tA = self.rot(es, "tA", [128, TB], BF16, 2)
        x3s = self.rot(es, "x3", [128, NTL, D], F32, 1)
        ss = self.rot(es, "f_ss", [128, 1], F32, 2)
        outs = self.rot(es, "outt", [128, D], F32, 1)
        wuT = Dm["wuT_bf"].rearrange("(c p) e -> p c e", p=128)
        b_h3T, b_ro, b_x3 = {}, {}, {}

        def load_ro(blk):
            ro = routs.next()
            self.load(ro.t[:], Dm["route_d"][NTL * blk:NTL * (blk + 1)].rearrange("a s j t -> s a j t"),
                      [DB["route_d"]], [ro.b])
            b_ro[blk] = ro

        def load_h3T(blk):
            h3T = h3Ts.next()
            tok0 = blk * TB
            self.load(h3T.t[:], Dm["h3T_d"][:, :, tok0:tok0 + TB].rearrange("c p t -> p c t"), [DB["h3T_d"]], [h3T.b])
            b_h3T[blk] = h3T

        def load_x3(blk):
            x3 = x3s.next()
            tok0 = blk * TB
            self.load(x3.t[:], Dm["x2_d"][tok0:tok0 + TB, :].rearrange("(t p) d -> p t d", p=128), [DB["x2_d"]], [x3.b],
                      queue=("sp" if blk == 0 else "act"))
            b_x3[blk] = x3

        def g_batches(u):
            blk, q = u // NQ, u % NQ
            if q == 0:
                load_ro(blk)
            ro = b_ro[blk]
            G = GtQ.next()
            rb = robs.next()
            st_ = {"G": G}

            def prep():
                self.ts("dve", rb.t[:, :, 0, :], ro.t[:, :, 0, :], float(-EQ * q), None, ALU.add, None, [ro.b], [rb.b])
                self.dve_copy(rb.t[:, :, 1:3, :], ro.t[:, :, 1:3, :], [ro.b], [rb.b])
            outl = [prep]
            for bi, t0 in enumerate(range(0, TB, GB)):
                def part1(bi=bi, t0=t0):
                    a_, tt_ = t0 // 128, t0 % 128
                    o1, o2 = oh1.next(), oh2.next()
                    ioq = iobf.t[:, 0:EQ].unsqueeze(1).to_broadcast([128, GB, EQ])
                    io128 = iobf.t[:].unsqueeze(1).to_broadcast([128, GB, 128])
                    e1b = rb.t[:, a_, 0, tt_:tt_ + GB].unsqueeze(2).to_broadcast([128, GB, EQ])
                    gb_ = rb.t[:, a_, 2, tt_:tt_ + GB].unsqueeze(2).to_broadcast([128, GB, EQ])
                    e2b = rb.t[:, a_, 1, tt_:tt_ + GB].unsqueeze(2).to_broadcast([128, GB, 128])
                    self.tt("dve", o1.t[:], ioq, e1b, ALU.is_equal, [iobf.b, rb.b], [o1.b])
                    self.tt("dve", o1.t[:], o1.t[:], gb_, ALU.mult, [o1.b, rb.b], [o1.b])
                    self.tt("dve", o2.t[:], io128, e2b, ALU.is_equal, [iobf.b, rb.b], [o2.b])

                    def part2(bi=bi, t0=t0, o1=o1, o2=o2):
                        bank = bi % 2
                        for j in range(GB):
                            self.mm(pf[:, bank * 512 + j * EQ: bank * 512 + (j + 1) * EQ], o2.t[:, j, :], o1.t[:, j, :],
                                    True, True, [o1.b, o2.b], [pfb[bank]])
                        self.act_copy(G.t[:, :, t0:t0 + GB].rearrange("p e t -> p t e"),
                                      pf[:, bank * 512:(bank + 1) * 512].rearrange("p (t e) -> p t e", t=GB),
                                      [pfb[bank]], [G.b])
                    return part2
                outl.append(part1)
            return st_, outl

        nun = (NT // TB) * NQ
        NG = EQ // 4
        pre_wu = {}

        def issue_wu(u_, g_):
            q_ = u_ % NQ
            wu_ = wus.next()
            e0_ = (q_ * EQ + 4 * g_) * 128
            self.load(wu_.t[:], wuT[:, :, e0_:e0_ + 512], [DB["wuT_bf"]], [wu_.b])
            pre_wu[(u_, g_)] = wu_

        pre_wv = {}

        def issue_wv(u_, dq_, g_):
            q_ = u_ % NQ
            wv_ = wvs.next()
            r0_ = (q_ * EQ + 4 * g_) * 128
            self.load(wv_.t[:], Dm["wv_bf"][r0_:r0_ + 512, dq_ * 512:(dq_ + 1) * 512]
                      .rearrange("(a p) d -> p a d", p=128), [DB["wv_bf"]], [wv_.b])
            pre_wv[(u_, dq_, g_)] = wv_

        fin_list = []
        load_h3T(0)
        load_x3(0)
        cur, cur_list = g_batches(0)
        for f in cur_list:
            r_ = f()
            if r_ is not None:
                r_()
        for u in range(nun):
            blk, q = u // NQ, u % NQ
            h3T = b_h3T[blk]
            G = cur["G"]
            if u + 1 < nun:
                nxt, nxt_list = g_batches(u + 1)
            else:
                nxt, nxt_list = None, []
            for g in range(NG):
                if (u, g) not in pre_wu:
                    issue_wu(u, g)
                wu = pre_wu.pop((u, g))
                if g == 1:
                    for g2 in range(3):
                        issue_wv(u, 0, g2)
                for el in range(4):
                    e1l = 4 * g + el
                    bank = 2 + e1l % 2
                    pend = None
                    if nxt_list:
                        if e1l == 0:
                            nxt_list.pop(0)()
                        pend = nxt_list.pop(0)()
                    for cc in range(16):
                        self.mm(pf[:, bank * 512:(bank + 1) * 512], wu.t[:, cc, el * 128:(el + 1) * 128], h3T.t[:, cc, :],
                                cc == 0, cc == 15, [wu.b, h3T.b], [pfb[bank]])
                    ta = tA.next()
                    P.op("act", lambda e, bank=bank, ta=ta: e.activation(
                        out=ta.t[:], in_=pf[:, bank * 512:(bank + 1) * 512], func=AF.Gelu_apprx_tanh),
                        reads=[pfb[bank]], writes=[ta.b])
                    self.tt("dve", G.t[:, e1l, :], G.t[:, e1l, :], ta.t[:], ALU.mult, [G.b, ta.b], [G.b])
                    if pend is not None:
                        pend()
                    if fin_list and e1l % 4 == 2:
                        fin_list.pop(0)()
            assert not nxt_list
            while fin_list:
                fin_list.pop(0)()
            x3 = b_x3[blk]
            if u + 1 < nun:
                issue_wu(u + 1, 0)
                issue_wu(u + 1, 1)
                if q == NQ - 1:
                    load_h3T(blk + 1)
            for dq in range(4):
                for g in range(NG):
                    if (u, dq, g) not in pre_wv:
                        issue_wv(u, dq, g)
                    wv = pre_wv.pop((u, dq, g))
                    for el in range(4):
                        e1l = 4 * g + el
                        for tl in range(NTL):
                            bank = 4 + tl
                            self.mm(pf[:, bank * 512:(bank + 1) * 512], G.t[:, e1l, tl * 128:(tl + 1) * 128],
                                    wv.t[:, el, :], e1l == 0, e1l == EQ - 1, [G.b, wv.b], [pfb[bank]])
                for tl in range(NTL):
                    dst = x3.t[:, tl, dq * 512:(dq + 1) * 512]
                    self.tt("dve", dst, dst, pf[:, (4 + tl) * 512:(5 + tl) * 512], ALU.add,
                            [x3.b, pfb[4 + tl]], [x3.b])
            if q == NQ - 1:
                tok0 = blk * TB

                def fin(tl, x3=x3, tok0=tok0):
                    s_ = ss.next()
                    o = outs.next()
                    P.op("act", lambda e, tl=tl, s_=s_, o=o, x3=x3: e.activation(out=o.t[:], in_=x3.t[:, tl, :],
                                                                                 func=AF.Square, accum_out=s_.t[:]),
                         reads=[x3.b], writes=[o.b, s_.b])
                    P.op("act", lambda e, s_=s_: e.activation(out=s_.t[:], in_=s_.t[:], func=AF.Sqrt, scale=1.0 / D,
                                                              bias=self.eps.t[:]),
                         reads=[s_.b, self.eps.b], writes=[s_.b])
                    P.op("dve", lambda e, s_=s_: e.reciprocal(out=s_.t[:], in_=s_.t[:]), reads=[s_.b], writes=[s_.b])
                    P.op("dve", lambda e, tl=tl, s_=s_, o=o, x3=x3: e.scalar_tensor_tensor(
                        out=o.t[:], in0=x3.t[:, tl, :], scalar=s_.t[:, 0:1], in1=gfin.t[:], op0=ALU.mult, op1=ALU.mult),
                        reads=[x3.b, s_.b, gfin.b], writes=[o.b])
                    row = tok0 + tl * 128
                    self.load(Dm["out"][row:row + 128, :], o.t[:], [o.b], [DB["out"]], cname="st", queue="act")
                for tl in range(NTL):
                    fin_list.append(lambda tl=tl, fin=fin: fin(tl))
                if u + 1 < nun:
                    fin_list.append(lambda blk=blk: load_x3(blk + 1))
            cur = nxt
        while fin_list:
            fin_list.pop(0)()


def _host_consts(half):
    bf = ml_dtypes.bfloat16
    c = {}
    c["ident_bf"] = np.eye(128, dtype=np.float32).astype(bf)
    c["ident_f"] = np.eye(128, dtype=np.float32)
    c["ones_bf"] = np.ones((128, 128), np.float32).astype(bf)
    kk = np.arange(128)[:, None]
    qq = np.arange(128)[None, :]
    cur = (kk <= qq).astype(np.float32)
    prev = (kk >= qq).astype(np.float32)
    prevh = prev * float(half)
    m = np.zeros((128, 3, 512), np.float32)
    m[:, 0] = np.concatenate([prev, cur, prev, cur], axis=1)
    m[:, 1] = np.concatenate([prevh, cur, prevh, cur], axis=1)
    m[:, 2] = np.concatenate([prevh, cur, prev, cur], axis=1)
    c["masks"] = m.astype(bf)
    A = np.zeros((128, 4, 3, 128), np.float32)
    s = np.arange(128)[:, None]
    t = np.arange(128)[None, :]
    for g, w in enumerate((2, 4, 8, 16)):
        diag = ((s <= t) & (s > t - w)).astype(np.float32) / w
        off = ((s + 128 - 128 <= 127) & (s - 128 > t - w)).astype(np.float32) / w
        first = diag.copy()
        if half == 0:
            cnt = np.minimum(t + 1, w).astype(np.float32)
            first = ((s <= t) & (s > t - w)).astype(np.float32) / cnt
        A[:, g, 0, :] = first - np.eye(128, dtype=np.float32)
        A[:, g, 1, :] = diag - np.eye(128, dtype=np.float32)
        A[:, g, 2, :] = off
    c["poolA"] = A.astype(bf)
    inv = (500000.0 ** (-np.arange(0, 32, 2, dtype=np.float32) / 32)).astype(np.float32)
    rc = np.zeros((128, 2, 32), np.float32)
    rc[:, 0, :] = np.concatenate([inv, inv])[None, :]
    rc[:, 1, 16:] = np.float32(np.pi / 2)
    c["rope_c"] = rc
    c["iota128"] = np.tile(np.arange(128, dtype=np.float32)[None, :], (128, 1))
    c["iota16"] = np.tile(np.arange(16, dtype=np.float32)[None, :], (128, 1))
    c["eps_c"] = np.full((128, 1), 1e-6, np.float32)
    return c


def make_in_maps(inp, cores=range(NCORES)):
    f = lambda a: np.ascontiguousarray(np.asarray(a))
    x = f(inp["x"]); mem = f(inp["mem"]); pos = f(inp["positions"]).astype(np.int32)
    shared = {
        "w_in": f(inp["w_in"][0]), "w_pool": f(inp["w_pool"][0]), "pool_scale": f(inp["pool_scale"][0]),
        "w_out": f(inp["w_out"][0]), "w_cq": f(inp["w_cq"][0]), "w_ck": f(inp["w_ck"][0]),
        "w_cv": f(inp["w_cv"][0]), "w_co": f(inp["w_co"][0]), "w_pq": f(inp["w_pq"][0]),
        "skT1": f(np.asarray(inp["sub_keys_1"][0]).T), "skT2": f(np.asarray(inp["sub_keys_2"][0]).T),
        "w_uT": f(np.asarray(inp["w_u"][0]).T), "w_v": f(inp["w_v"][0]),
        "g_final": f(np.asarray(inp["g_final"]).reshape(1, D)),
    }
    for k, src in (("g_mix_t", "g_mix"), ("g_cross_t", "g_cross"), ("g_mem_t", "g_mem"), ("g_ffn_t", "g_ffn")):
        shared[k] = f(np.asarray(inp[src][0]).reshape(16, 128).T)
    consts = [_host_consts(0), _host_consts(1)]
    maps = []
    for c in cores:
        b, half = c // 2, c % 2
        m = dict(shared)
        m.update(consts[half])
        s0 = half * NT
        m["x_own"] = f(x[b, s0:s0 + NT])
        if half == 0:
            m["x_halo"] = np.zeros((NH, D), np.float32)
            ph = np.zeros((NH,), np.int32)
        else:
            m["x_halo"] = f(x[b, s0 - NH:s0])
            ph = pos[b, s0 - NH:s0]
        pa = np.concatenate([ph, pos[b, s0:s0 + NT]])
        m["pos_t"] = f(pa.reshape(32, 128).T)
        m["mem"] = f(mem[b])
        maps.append(m)
    return maps


_NC_CACHE = {}


def kernel(**inputs):
    if "nc" not in _NC_CACHE:
        _NC_CACHE["nc"] = KB().build()
    nc = _NC_CACHE["nc"]
    maps = make_in_maps(inputs)
    res = run_bass_kernel_spmd(nc, maps, core_ids=list(range(NCORES)))
    out = np.empty((4, 4096, D), np.float32)
    for c in range(NCORES):
        b, half = c // 2, c % 2
        out[b, half * NT:(half + 1) * NT] = np.asarray(res.results[c]["out"], dtype=np.float32)
    return out
```
